# Optimizing a Trainium2 kernel written in Bass

```python
import math
import jax, jax.numpy as jnp
from jax import lax
import numpy as np


D_MODEL = 1024
BATCH = 16
SEQ = 2048
DEPTH = 2

GRID_W = 64
CTX_LEN = 256
N_GROUPS = 4
GROUP_W = D_MODEL // N_GROUPS
HEADS = 4
HEAD_DIM = GROUP_W // HEADS
CHUNK = 128
DIFF_QK = HEAD_DIM // 2
ROT_HALF = DIFF_QK // 2
ROPE_THETA = 10000.0
Q_BLOCK = 128
D_FF = 2816
EPS = 1e-6
G_AU, G_AV, G_BX, G_BB, G_BC, G_CF, G_DQ, G_DK, G_DV = range(9)
IN_COLS = 9 * GROUP_W
KV_COL0 = G_DK * GROUP_W

kernel_name = 'hybrid_parallel_head_dit_block'


def rmsnorm(x, g=None):
    xf = x.astype(jnp.float32)
    y = xf * lax.rsqrt(jnp.mean(xf * xf, axis=-1, keepdims=True) + EPS)
    if g is not None:
        y = y * g.astype(jnp.float32)
    return y.astype(x.dtype)


def adaln_params(cvec, w_ada, b_ada):
    m = jax.nn.silu(cvec) @ w_ada + b_ada
    return jnp.split(m, 6, axis=-1)


def modulate(h, shift, scale):
    return h * (1 + scale) + shift


def group(z, i):
    return z[..., i * GROUP_W:(i + 1) * GROUP_W]


def dwconv3(x, w):
    xp = jnp.pad(x, ((0, 0), (1, 1), (0, 0)))
    return xp[:, :-2] * w[0] + xp[:, 1:-1] * w[1] + xp[:, 2:] * w[2]


def mixer_a(z, ws, bs):
    B, N, _ = z.shape
    u = jax.nn.gelu(group(z, G_AU))
    v = rmsnorm(jax.nn.gelu(group(z, G_AV)).reshape(B, N, HEADS, HEAD_DIM))
    v = v.reshape(B, N // CHUNK, CHUNK, HEADS, HEAD_DIM)
    mixed = jnp.einsum('hpq,bcqhd->bcphd', ws, v) + bs.T[:, :, None]
    return u * mixed.reshape(B, N, GROUP_W)


def mixer_b(z, w):
    return group(z, G_BB) * dwconv3(group(z, G_BC) * group(z, G_BX), w)


def mixer_c(z):
    B, N, _ = z.shape
    f = group(z, G_CF).reshape(B, N, HEADS, HEAD_DIM).astype(jnp.float32)
    F = jnp.fft.fft2(f, axes=(1, 3), norm='ortho').real
    return F.astype(z.dtype).reshape(B, N, GROUP_W)


def apply_rope_2d(t, ang_r, ang_c):
    def rot(u, ang):
        cos = jnp.cos(ang)[None, :, None, None, :].astype(u.dtype)
        sin = jnp.sin(ang)[None, :, None, None, :].astype(u.dtype)
        u1, u2 = u[..., :ROT_HALF // 2], u[..., ROT_HALF // 2:]
        return jnp.concatenate([u1 * cos - u2 * sin, u2 * cos + u1 * sin], axis=-1)
    return jnp.concatenate([rot(t[..., :ROT_HALF], ang_r), rot(t[..., ROT_HALF:], ang_c)], axis=-1)


def diff_attend(q, k, v, lam):
    s = jnp.einsum('bqhmd,bkhmd->bhmqk', q, k).astype(jnp.float32) * (DIFF_QK ** -0.5)
    p = jax.nn.softmax(s, axis=-1)
    a = p[:, :, 0] - lam * p[:, :, 1]
    return jnp.einsum('bhqk,bkhd->bqhd', a.astype(v.dtype), v)


def diff_out(o, g, lam_init):
    B, N = o.shape[0], o.shape[1]
    return (rmsnorm(o, g) * (1 - lam_init)).reshape(B, N, GROUP_W)


def conv_ffn(h, w_up, conv_w, w_down):
    up = h @ w_up
    a, b = up[..., :D_FF], up[..., D_FF:]
    return (jax.nn.silu(dwconv3(a, conv_w)) * b) @ w_down


def trunk_layer(x, xc, c, c_ctx, ang_r, ang_c, layer_idx, ctx_out,
                w_ada, b_ada, g1, g2, w_in, gm_ws, gm_bs, sc_w,
                lq1, lk1, lq2, lk2, subln_g, w_out, w_up, ffn_conv, w_down):
    B, N, _ = x.shape
    L = xc.shape[1]
    sh1, sc1, gt1, sh2, sc2, gt2 = [m[:, None, :] for m in adaln_params(c, w_ada, b_ada)]
    shc1, scc1, gtc1, shc2, scc2, gtc2 = adaln_params(c_ctx, w_ada, b_ada)
    lam_init = 0.8 - 0.6 * math.exp(-0.3 * layer_idx)
    f32 = jnp.float32
    lam = (jnp.exp(jnp.sum(lq1.astype(f32) * lk1.astype(f32)))
           - jnp.exp(jnp.sum(lq2.astype(f32) * lk2.astype(f32))) + lam_init)

    h = modulate(rmsnorm(x, g1), sh1, sc1)
    hc = modulate(rmsnorm(xc, g1), shc1, scc1)
    z = h @ w_in
    zc_kv = hc @ w_in[:, KV_COL0:]
    kc = zc_kv[..., :GROUP_W].reshape(B, L, HEADS, 2, DIFF_QK)
    vc = zc_kv[..., GROUP_W:].reshape(B, L, HEADS, HEAD_DIM)

    q = apply_rope_2d(group(z, G_DQ).reshape(B, N, HEADS, 2, DIFF_QK), ang_r, ang_c)
    k = apply_rope_2d(group(z, G_DK).reshape(B, N, HEADS, 2, DIFF_QK), ang_r, ang_c)
    v = group(z, G_DV).reshape(B, N, HEADS, HEAD_DIM)
    k_all = jnp.concatenate([kc, k], axis=1)
    v_all = jnp.concatenate([vc, v], axis=1)
    qb = q.reshape(B, N // Q_BLOCK, Q_BLOCK, HEADS, 2, DIFF_QK).swapaxes(0, 1)
    o = lax.map(lambda qi: diff_attend(qi, k_all, v_all, lam), qb)
    o = o.swapaxes(0, 1).reshape(B, N, HEADS, HEAD_DIM)

    mix = jnp.concatenate([mixer_a(z, gm_ws, gm_bs), mixer_b(z, sc_w), mixer_c(z),
                           diff_out(o, subln_g, lam_init)], axis=-1)
    x = x + gt1 * (mix @ w_out)
    x = x + gt2 * conv_ffn(modulate(rmsnorm(x, g2), sh2, sc2), w_up, ffn_conv, w_down)
    if not ctx_out:
        return x, None

    zc = hc @ w_in[:, :KV_COL0]
    qc = group(zc, G_DQ).reshape(B, L, HEADS, 2, DIFF_QK)
    oc = diff_attend(qc, kc, vc, lam)
    mixc = jnp.concatenate([mixer_a(zc, gm_ws, gm_bs), mixer_b(zc, sc_w), mixer_c(zc),
                            diff_out(oc, subln_g, lam_init)], axis=-1)
    xc = xc + gtc1 * (mixc @ w_out)
    xc = xc + gtc2 * conv_ffn(modulate(rmsnorm(xc, g2), shc2, scc2), w_up, ffn_conv, w_down)
    return x, xc


def setup_inputs(seed: int = 0) -> dict:
    key = jax.random.key(seed)
    ks = jax.random.split(key, 24)
    D = D_MODEL

    def nrm(k, shape, scale):
        return jax.random.normal(k, shape, jnp.float32) * scale

    return {
        'x': nrm(ks[0], (BATCH, SEQ, D), 1.0),
        'c': nrm(ks[1], (BATCH, D), 1.0),
        'ctx': nrm(ks[2], (BATCH, CTX_LEN, D), 1.0),
        'c_ctx': nrm(ks[3], (D,), 1.0),
        'w_ada': nrm(ks[4], (DEPTH, D, 6 * D), 0.5 * D ** -0.5),
        'b_ada': nrm(ks[5], (DEPTH, 6 * D), 0.02),
        'norm1_g': 1.0 + nrm(ks[6], (DEPTH, D), 0.02),
        'norm2_g': 1.0 + nrm(ks[7], (DEPTH, D), 0.02),
        'w_in': nrm(ks[8], (DEPTH, D, IN_COLS), D ** -0.5),
        'gmlp_ws': nrm(ks[9], (DEPTH, HEADS, CHUNK, CHUNK), CHUNK ** -0.5),
        'gmlp_bs': 1.0 + nrm(ks[10], (DEPTH, HEADS, CHUNK), 0.02),
        'sconv_w': nrm(ks[11], (DEPTH, 3, GROUP_W), 3 ** -0.5),
        'lambda_q1': nrm(ks[12], (DEPTH, DIFF_QK), 0.1),
        'lambda_k1': nrm(ks[13], (DEPTH, DIFF_QK), 0.1),
        'lambda_q2': nrm(ks[14], (DEPTH, DIFF_QK), 0.1),
        'lambda_k2': nrm(ks[15], (DEPTH, DIFF_QK), 0.1),
        'subln_g': 1.0 + nrm(ks[16], (DEPTH, HEAD_DIM), 0.02),
        'w_out': nrm(ks[17], (DEPTH, D, D), D ** -0.5),
        'ffn_w_up': nrm(ks[18], (DEPTH, D, 2 * D_FF), D ** -0.5),
        'ffn_conv_w': nrm(ks[19], (DEPTH, 3, D_FF), 3 ** -0.5),
        'ffn_w_down': nrm(ks[20], (DEPTH, D_FF, D), D_FF ** -0.5),
        'final_g': 1.0 + nrm(ks[21], (D,), 0.02),
    }


def reference(x, c, ctx, c_ctx, w_ada, b_ada, norm1_g, norm2_g, w_in, gmlp_ws, gmlp_bs,
              sconv_w, lambda_q1, lambda_k1, lambda_q2, lambda_k2, subln_g, w_out,
              ffn_w_up, ffn_conv_w, ffn_w_down, final_g):
    N = x.shape[1]
    ROWS = N // GRID_W
    row = jnp.broadcast_to(jnp.arange(ROWS, dtype=jnp.float32)[:, None], (ROWS, GRID_W)).reshape(-1)
    col = jnp.broadcast_to(jnp.arange(GRID_W, dtype=jnp.float32)[None, :], (ROWS, GRID_W)).reshape(-1)
    inv_freq = ROPE_THETA ** (-jnp.arange(0, ROT_HALF, 2, dtype=jnp.float32) / ROT_HALF)
    ang_r = row[:, None] * inv_freq
    ang_c = col[:, None] * inv_freq
    xc = ctx
    for l in range(DEPTH):
        x, xc = trunk_layer(
            x, xc, c, c_ctx, ang_r, ang_c, l, l < DEPTH - 1,
            w_ada[l], b_ada[l], norm1_g[l], norm2_g[l], w_in[l], gmlp_ws[l], gmlp_bs[l],
            sconv_w[l], lambda_q1[l], lambda_k1[l], lambda_q2[l], lambda_k2[l], subln_g[l],
            w_out[l], ffn_w_up[l], ffn_conv_w[l], ffn_w_down[l])
    return rmsnorm(x, final_g)
```

```python
import math
from contextlib import ExitStack
import numpy as np
import ml_dtypes
import concourse.bass as bass
import concourse.mybir as mybir
from concourse.bass_utils import run_bass_kernel_spmd

F32 = mybir.dt.float32
BF16 = mybir.dt.bfloat16
AF = mybir.ActivationFunctionType
ALU = mybir.AluOpType
AX = mybir.AxisListType

D = 1024
NTOK = 2048
LCTX = 256
DFF = 2816
NFF = DFF // 128
EPS = 1e-6
NB_PER_CORE = 2
N_CORES = 8
WARM = True


class Buf:
    __slots__ = ("name", "writers", "readers", "dsem")

    def __init__(self, name):
        self.name = name
        self.writers = {}
        self.readers = {}
        self.dsem = None


class Sync:
    EPOCH = 30000

    def __init__(self, nc, stack):
        self.nc = nc
        self.stack = stack
        self.eng = {"pe": nc.tensor, "act": nc.scalar, "dve": nc.vector,
                    "pool": nc.gpsimd, "sp": nc.sync}
        self.sems = {}
        self.total = {}
        self.cur = {}
        self.known = {e: {} for e in self.eng}
        self.pending = {e: [] for e in self.eng}
        self.nsem = 0
        for e in self.eng:
            self._new_engine_sem(e)
        self.ninst = {e: 0 for e in self.eng}
        self.halt = False

    def _alloc(self, name):
        h = self.stack.enter_context(self.nc.semaphore(name))
        self.nsem += 1
        return h

    def _new_engine_sem(self, e):
        key = ("E", e, self.nsem)
        self.sems[key] = self._alloc(f"s_{e}_{self.nsem}")
        self.total[key] = 0
        self.cur[e] = key

    def dma_sem(self, buf):
        if buf.dsem is None:
            key = ("D", buf.name)
            if key not in self.sems:
                self.sems[key] = self._alloc(f"d_{self.nsem}")
                self.total[key] = 0
            buf.dsem = key
        return buf.dsem

    def share_dsem(self, bufs):
        k = self.dma_sem(bufs[0])
        for b in bufs[1:]:
            b.dsem = k

    def _wait(self, e, need):
        engine = self.eng[e]
        for key, val in need.items():
            if key[0] == "D":
                val = self.total[key]
            if self.known[e].get(key, 0) >= val:
                continue
            engine.wait_ge(self.sems[key], val)
            self.known[e][key] = val

    def _collect(self, e, reads, writes):
        need = {}
        for b in reads:
            for k, v in b.writers.items():
                if need.get(k, 0) < v:
                    need[k] = v
        for b in writes:
            for k, v in list(b.writers.items()) + list(b.readers.items()):
                if k[0] == "E" and k[1] == e and e != "pool":
                    continue
                if need.get(k, 0) < v:
                    need[k] = v
        return need

    def _register(self, tok, reads, writes):
        k, v = tok
        for b in reads:
            if b.readers.get(k, 0) < v:
                b.readers[k] = v
        for b in writes:
            b.writers = {k: v}
            b.readers = {}

    def op(self, e, fn, reads=(), writes=(), inc=True):
        if self.halt:
            return None
        reads = list(reads)
        writes = list(writes)
        self._wait(e, self._collect(e, reads, writes))
        inst = fn(self.eng[e])
        self.ninst[e] += 1
        if not inc:
            self.pending[e].append((reads, writes))
            return inst
        key = self.cur[e]
        self.total[key] += 1
        inst.then_inc(self.sems[key], 1)
        tok = (key, self.total[key])
        for (r, w) in self.pending[e]:
            self._register(tok, r, w)
        self.pending[e] = []
        self._register(tok, reads, writes)
        if self.total[key] >= self.EPOCH:
            self._new_engine_sem(e)
        return inst

    def dma(self, e, out, in_, reads=(), writes=(), **kw):
        if self.halt:
            return None
        reads = list(reads)
        writes = list(writes)
        assert not self.pending[e]
        self._wait(e, self._collect(e, reads, writes))
        key = self.dma_sem(writes[0])
        inst = self.eng[e].dma_start(out=out, in_=in_, **kw)
        inst.then_inc(self.sems[key], 16)
        self.total[key] += 16
        self.ninst[e] += 1
        self._register((key, self.total[key]), reads, writes)
        return inst

    def barrier(self):
        if self.halt:
            return
        for e in self.eng:
            assert not self.pending[e], e
        need = {k: v for k, v in self.total.items() if v > 0}
        for e in self.eng:
            self._wait(e, dict(need))

    def final_wait(self, e, bufs):
        need = {}
        for b in bufs:
            for k, v in b.writers.items():
                need[k] = max(need.get(k, 0), v)
        self._wait(e, need)


class TT:
    def __init__(self, name, t, nchunk, ntok, blk):
        self.t = t
        self.nchunk = nchunk
        self.ntok = ntok
        self.blk = blk
        self.nblk = (ntok + blk - 1) // blk
        self.b = [[Buf(f"{name}_{c}_{j}") for j in range(self.nblk)] for c in range(nchunk)]

    def bufs(self, c0, c1, t0, t1):
        j0 = t0 // self.blk
        j1 = (t1 - 1) // self.blk + 1
        return [self.b[c][j] for c in range(c0, c1) for j in range(j0, j1)]


def _consts():
    bf = ml_dtypes.bfloat16
    c = {}
    c["ident"] = np.eye(128, dtype=np.float32)
    d = np.arange(64)
    ang = 2 * np.pi * ((d[:, None] * d[None, :]) % 64) / 64.0
    c64 = np.zeros((128, 2, 128), np.float64)
    for hh in range(2):
        c64[hh * 64:(hh + 1) * 64, 0, hh * 64:(hh + 1) * 64] = np.cos(ang)
        c64[hh * 64:(hh + 1) * 64, 1, hh * 64:(hh + 1) * 64] = -np.sin(ang)
    c["c64"] = c64.astype(bf)
    rp = np.zeros((128, 128), np.float32)
    for cp in range(128):
        j = cp % 16
        partner = cp + 8 if j < 8 else cp - 8
        rp[partner, cp] = 1.0
    c["rperm"] = rp.astype(bf)
    n = np.arange(NTOK)
    row = (n // 64).astype(np.float32)
    col = (n % 64).astype(np.float32)
    inv_freq = (np.float32(10000.0) ** (-np.arange(0, 16, 2, dtype=np.float32) / np.float32(16))).astype(np.float32)
    rope = np.zeros((128, 2, NTOK), np.float32)
    for p in range(128):
        dq = p % 32
        pos = row if dq < 16 else col
        j = dq % 16
        a = (pos * inv_freq[j % 8]).astype(np.float32)
        rope[p, 0] = np.cos(a)
        rope[p, 1] = -np.sin(a) if j < 8 else np.sin(a)
    c["rope"] = rope.astype(bf)
    def dftn(N, kw):
        nn = np.arange(N)
        kk = np.arange(N)
        a = 2 * np.pi * ((nn[:, None] * kk[None, :]) % N) / float(N)
        s = 1.0 / math.sqrt(64.0 * N)
        C = (np.cos(a) * s).reshape(N // 128, 128, N // kw, kw)
        Sn = (np.sin(a) * s).reshape(N // 128, 128, N // kw, kw)
        out = np.stack([C, Sn], axis=0)
        return np.ascontiguousarray(out.transpose(3, 2, 1, 0, 4)).astype(bf)
    c["cn"] = dftn(NTOK, 512)
    c["cn256"] = dftn(LCTX, 256)[0]
    return c


_CONSTS = None


def _get_consts():
    global _CONSTS
    if _CONSTS is None:
        _CONSTS = _consts()
    return _CONSTS


class _Stop(Exception):
    pass


def build_program(nseq=NB_PER_CORE, nlayers=2, dbg=None, stop=None):
    dbg = dbg or {}

    def chk(tag):
        if stop == tag:
            SREF[0].halt = True

    SREF = [None]
    nc = bass.Bass("TRN2", target_bir_lowering=False)

    def din(name, shape, dt=F32):
        return nc.dram_tensor(name, list(shape), dt, kind="ExternalInput").ap()

    x_d = din("x", [nseq, NTOK, D])
    ctx_d = din("ctx", [nseq, LCTX, D])
    cT_d = din("cT", [128, 8, 3])
    wada_d = din("w_ada", [2, D, 6 * D])
    badaT_d = din("b_adaT", [128, 2, 48])
    gT_d = din("gT", [128, 5, 8])
    win_d = din("w_in", [2, D, 2304])
    wout_d = din("w_out", [2, D, D])
    wup_d = din("w_up", [2, D, 2 * DFF])
    wdown_d = din("w_down", [2, DFF, D])
    wsT_d = din("wsT", [128, 2, 4, 128])
    gbiasT_d = din("gbiasT", [128, 2, 2, 128])
    sconvT_d = din("sconvT", [128, 2, 2, 3])
    fconvT_d = din("fconvT", [128, 2, NFF, 3])
    sublnT_d = din("sublnT", [128, 2])
    lamv_d = din("lamv", [128, 2, 4, 32])
    ident_d = din("ident", [128, 128])
    c64_d = din("c64", [128, 2, 128], BF16)
    rperm_d = din("rperm", [128, 128], BF16)
    rope_d = din("rope", [128, 2, NTOK], BF16)
    cn_d = din("cn", [4, 128, 16, 2, 512], BF16)
    cn256_d = din("cn256", [128, 2, 2, 256], BF16)
    out_d = nc.dram_tensor("out", [nseq, NTOK, D], F32, kind="ExternalOutput").ap()
    scr_d = nc.dram_tensor("scr_rows", [2, 512], F32).ap()
    win_f = nc.dram_tensor("win_f", [2, D, 2304], BF16).ap()
    wout_f = nc.dram_tensor("wout_f", [2, D, D], BF16).ap()
    wup_f = nc.dram_tensor("wup_f", [2, D, 2 * DFF], BF16).ap()
    wdown_s = nc.dram_tensor("wdown_s", [2, 8, 128, NFF, 128], BF16).ap()
    dbg_d = {}
    for name, shape in dbg.items():
        dbg_d[name] = nc.dram_tensor("dbg_" + name, list(shape), F32, kind="ExternalOutput").ap()

    lam_init = [0.8 - 0.6 * math.exp(-0.3 * l) for l in range(2)]

    with ExitStack() as st:
        S = Sync(nc, st)
        SREF[0] = S

        tcount = [0]

        def T(stack, name, shape, dt):
            tcount[0] += 1
            return stack.enter_context(nc.sbuf_tensor(f"sb{tcount[0]}_{name}", list(shape), dt))

        ps = st.enter_context(nc.psum_tensor("ps", [128, 8, 512], F32))
        psb = [Buf(f"ps{i}") for i in range(8)]
        outB = Buf("out")
        scr_b = [Buf("scr0"), Buf("scr1")]
        wscr_b = {k: Buf("wscr_" + k) for k in ("in", "out", "up", "down")}
        converted = set()

        def wload(kind, l_, idx, slot, slot_b, scratch_ap, cast_loads):
            key = (kind, l_, idx)
            if S.halt:
                return
            if key not in converted:
                for (dst, src) in cast_loads:
                    S.dma("pool", dst, src, writes=[slot_b])
                S.dma("sp", scratch_ap, slot, reads=[slot_b], writes=[wscr_b[kind]])
                converted.add(key)
            else:
                S.dma("sp", slot, scratch_ap, reads=[wscr_b[kind]], writes=[slot_b])
        dbgB = Buf("dbgout")

        xT_t = T(st, "xT", [128, 8, NTOK], F32)
        xT = TT("xT", xT_t, 8, NTOK, 512)
        xcT_t = T(st, "xcT", [128, 8, LCTX], F32)
        xcT = TT("xcT", xcT_t, 8, LCTX, 256)
        ctxK = T(st, "ctxK", [128, 2, LCTX], BF16); ctxK_b = Buf("ctxK")
        ctxV = T(st, "ctxV", [128, 2, 384], BF16); ctxV_b = Buf("ctxV")
        rope = T(st, "rope", [128, 2, NTOK], BF16); rope_b = Buf("rope")
        ident = T(st, "ident", [128, 128], F32); ident_b = Buf("ident")
        ones_bf = T(st, "ones_bf", [128, 128], BF16); ones_b = Buf("ones")
        coef = T(st, "coef", [128, 2, 2, 128], F32); coef_b = Buf("coef")
        c64 = T(st, "c64", [128, 2, 128], BF16); c64_b = Buf("c64")
        rperm = T(st, "rperm", [128, 128], BF16); rperm_b = Buf("rperm")
        wsT = T(st, "wsT", [128, 2, 4, 128], BF16); wsT_b = Buf("wsT")
        gbias = T(st, "gbias", [128, 2, 2, 128], F32); gbias_b = Buf("gbias")
        sconv = T(st, "sconv", [128, 2, 2, 3], F32); sconv_b = Buf("sconv")
        fconv = T(st, "fconv", [128, 2, NFF, 3], F32); fconv_b = Buf("fconv")
        subln = T(st, "subln", [128, 2], F32); subln_b = Buf("subln")
        gT = T(st, "gT", [128, 5, 8], F32); gT_b = Buf("gT")
        modT = T(st, "modT", [128, 2, 48, 3], F32); mod_b = Buf("modT")
        scg = T(st, "scg", [128, 2, 2, 8, 3], F32)
        epsT = T(st, "epsT", [128, 1], F32); eps_b = Buf("eps")
        neglam = T(st, "neglam", [128, 2], F32)
        zfill = T(st, "zfill", [128, 512], BF16); zfill_b = Buf("zfill")

        pe, act, dve, pool = "pe", "act", "dve", "pool"

        def mm(out, lhsT, rhs, start, stop, reads, writes, inc=None, tp=None):
            if inc is None:
                inc = stop
            kw = {}
            if tp is not None:
                kw["tile_position"] = tp
            return S.op(pe, lambda e: e.matmul(out, lhsT, rhs, start=start, stop=stop, **kw),
                        reads=reads, writes=writes, inc=inc)

        def A(out, in_, func, reads, writes, bias=None, scale=None):
            kw = {}
            if bias is not None:
                kw["bias"] = bias
            if scale is not None:
                kw["scale"] = scale
            return S.op(act, lambda e: e.activation(out, in_, func, **kw), reads=reads, writes=writes)

        def TTop(eng, out, in0, in1, op, reads, writes):
            return S.op(eng, lambda e: e.tensor_tensor(out, in0, in1, op), reads=reads, writes=writes)

        def STT(eng, out, in0, scalar, in1, op0, op1, reads, writes):
            return S.op(eng, lambda e: e.scalar_tensor_tensor(out, in0, scalar, in1, op0, op1),
                        reads=reads, writes=writes)

        def TS(eng, out, in0, s1, s2, op0, op1, reads, writes):
            if op1 is None:
                return S.op(eng, lambda e: e.tensor_scalar(out, in0, s1, None, op0), reads=reads, writes=writes)
            return S.op(eng, lambda e: e.tensor_scalar(out, in0, s1, s2, op0, op1), reads=reads, writes=writes)

        def CP(eng, out, in_, reads, writes):
            if eng == act:
                return S.op(act, lambda e: e.copy(out, in_), reads=reads, writes=writes)
            return S.op(eng, lambda e: e.tensor_copy(out, in_), reads=reads, writes=writes)

        def dump(name, src_ap, reads):
            if name in dbg_d:
                S.dma("pool", dbg_d[name], src_ap, reads=reads, writes=[dbgB])

        conv_b = {}

        def issue_conversions(l_):
            for kind, src_d, dst_d in (("in", win_d, win_f), ("out", wout_d, wout_f), ("up", wup_d, wup_f)):
                conv_b[(kind, l_)] = Buf(f"conv_{kind}_{l_}")
                S.dma("pool", dst_d[l_], src_d[l_], writes=[conv_b[(kind, l_)]])

        def wload_f(kind, l_, slot, slot_b, src_ap):
            if S.halt:
                return
            S.dma("sp", slot, src_ap, reads=[conv_b[(kind, l_)]], writes=[slot_b])

        const_bufs = [rope_b, ident_b, c64_b, rperm_b, gbias_b, sconv_b, fconv_b, subln_b, gT_b]
        S.share_dsem(const_bufs)
        S.dma("sp", rope[:], rope_d, writes=[rope_b])
        S.dma("sp", ident[:], ident_d, writes=[ident_b])
        S.dma("sp", c64[:], c64_d, writes=[c64_b])
        S.dma("sp", rperm[:], rperm_d, writes=[rperm_b])
        S.dma("sp", gbias[:], gbiasT_d, writes=[gbias_b])
        S.dma("sp", sconv[:], sconvT_d, writes=[sconv_b])
        S.dma("sp", fconv[:], fconvT_d, writes=[fconv_b])
        S.dma("sp", subln[:], sublnT_d, writes=[subln_b])
        S.dma("sp", gT[:], gT_d, writes=[gT_b])
        S.dma("pool", wsT[:], wsT_d, writes=[wsT_b])
        issue_conversions(0)
        S.op(dve, lambda e: e.memset(ones_bf[:], 1.0), writes=[ones_b])
        S.op(dve, lambda e: e.memset(zfill[:], 0.0), writes=[zfill_b])
        S.op(dve, lambda e: e.memset(epsT[:], EPS), writes=[eps_b])
        S.op(dve, lambda e: e.memset(coef[:], 1.0), writes=[coef_b])

        with ExitStack() as pro:
            lamv = T(pro, "lamv", [128, 2, 4, 32], F32); lamv_b = Buf("lamv")
            lprod = T(pro, "lprod", [128, 2, 2, 32], F32)
            lsum = T(pro, "lsum", [128, 2, 2], F32)
            lam = T(pro, "lam", [128, 2], F32); lam_b = Buf("lam")
            S.dma("sp", lamv[:], lamv_d, writes=[lamv_b])
            TTop(dve, lprod[:], lamv[:, :, 0::2, :], lamv[:, :, 1::2, :], ALU.mult, [lamv_b], [lam_b])
            S.op(dve, lambda e: e.tensor_reduce(lsum[:], lprod[:], AX.X, ALU.add), reads=[lam_b], writes=[lam_b])
            A(lsum[:], lsum[:], AF.Exp, [lam_b], [lam_b])
            TTop(dve, lam[:], lsum[:, :, 0], lsum[:, :, 1], ALU.subtract, [lam_b], [lam_b])
            for l in range(2):
                TS(dve, lam[:, l:l + 1], lam[:, l:l + 1], -1.0, -lam_init[l], ALU.mult, ALU.add, [lam_b], [lam_b])
                TS(dve, coef[:, l, 1, :], coef[:, l, 1, :], lam[:, l:l + 1], None, ALU.mult, None,
                   [lam_b, coef_b], [coef_b])
                CP(dve, neglam[:, l:l + 1], lam[:, l:l + 1], [lam_b], [coef_b])
                TS(dve, subln[:, l:l + 1], subln[:, l:l + 1], 1.0 - lam_init[l], None, ALU.mult, None,
                   [subln_b], [subln_b])

            cT = T(pro, "cT", [128, 8, 3], F32); cT_b = Buf("cT")
            siluc = T(pro, "siluc", [128, 8, 3], F32)
            badaT = T(pro, "badaT", [128, 2, 48], F32); bada_b = Buf("bada")
            S.dma("sp", cT[:], cT_d, writes=[cT_b])
            S.dma("sp", badaT[:], badaT_d, writes=[bada_b])
            A(siluc[:], cT[:], AF.Silu, [cT_b], [cT_b])
            wa = [T(pro, f"wa{i}", [128, 8, 1024], F32) for i in range(2)]
            wa_b = [Buf(f"wa{i}") for i in range(2)]
            it = 0
            for l in range(nlayers):
                for j in range(6):
                    sl = it % 2
                    src = wada_d[l].rearrange("(k p) c -> p k c", p=128)[:, :, j * 1024:(j + 1) * 1024]
                    S.dma("sp", wa[sl][:], src, writes=[wa_b[sl]])
                    bank = it % 2
                    for cc in range(8):
                        for k in range(8):
                            mm(ps[:, bank, cc * 3:cc * 3 + 3], wa[sl][:, k, cc * 128:(cc + 1) * 128], siluc[:, k, :],
                               k == 0, k == 7, [wa_b[sl], cT_b], [psb[bank]])
                    TTop(dve, modT[:, l, j * 8:(j + 1) * 8, :],
                         ps[:, bank, 0:24].rearrange("p (c s) -> p c s", s=3),
                         badaT[:, l, j * 8:(j + 1) * 8].unsqueeze(2).to_broadcast([128, 8, 3]),
                         ALU.add, [psb[bank], bada_b], [mod_b])
                    it += 1
                for which, off in ((0, 8), (1, 32)):
                    TS(dve, scg[:, l, which], modT[:, l, off:off + 8, :], 1.0, None, ALU.add, None, [mod_b], [mod_b])
                    TTop(dve, scg[:, l, which], scg[:, l, which],
                         gT[:, 2 * which + l, :].unsqueeze(2).to_broadcast([128, 8, 3]), ALU.mult,
                         [mod_b, gT_b], [mod_b])
            S.barrier()
        dump("modT", modT[:], [mod_b])

        def mod(l, j, fc, s):
            return modT[:, l, j * 8 + fc, s:s + 1]

        NB = {}

        def norm_mod(*args):
            for _ in norm_mod_g(*args):
                pass

        def norm_mod_g(src: TT, t0, w, dst_ap_fn, dst_bufs_fn, scA, shA, bank, tmp):
            sqt, tx, rstd_l, tb = tmp
            if not isinstance(rstd_l, list):
                rstd_l = [rstd_l]
            if id(tb) not in NB:
                NB[id(tb)] = [Buf("nm_sq"), [Buf(f"nm_rstd{i}") for i in range(len(rstd_l))],
                              [Buf("nm_tx0"), Buf("nm_tx1")], 0]
            nbe = NB[id(tb)]
            sq_b, tx_b = nbe[0], nbe[2]
            ri_ = nbe[3] % len(rstd_l)
            nbe[3] += 1
            rstd = rstd_l[ri_]
            rs_b = nbe[1][ri_]
            rb = src.bufs(0, 8, t0, t0 + w)
            A(sqt[:, :, 0:w], src.t[:, :, t0:t0 + w], AF.Square, rb, [sq_b])
            yield
            for fc in range(8):
                mm(ps[:, bank, 0:w], ones_bf[:], sqt[:, fc, 0:w], fc == 0, fc == 7, [ones_b, sq_b], [psb[bank]])
            A(rstd[:, 0:w], ps[:, bank, 0:w], AF.Ln, [psb[bank], eps_b], [rs_b], bias=epsT[:, 0:1], scale=1.0 / D)
            A(rstd[:, 0:w], rstd[:, 0:w], AF.Exp, [rs_b], [rs_b], scale=-0.5)
            yield
            for fc in range(8):
                if fc > 0:
                    yield
                eng = dve
                txi = tx[fc % 2]
                TTop(eng, txi[:, 0:w], src.t[:, fc, t0:t0 + w], rstd[:, 0:w], ALU.mult,
                     src.bufs(fc, fc + 1, t0, t0 + w) + [rs_b], [tx_b[fc % 2]])
                if shA is not None:
                    A(dst_ap_fn(fc), txi[:, 0:w], AF.Identity, [tx_b[fc % 2], mod_b], dst_bufs_fn(fc), bias=shA(fc), scale=scA(fc))
                else:
                    A(dst_ap_fn(fc), txi[:, 0:w], AF.Identity, [tx_b[fc % 2], gT_b], dst_bufs_fn(fc), scale=scA(fc))

        def load_seq(src_d, N, dst: TT):
            with ExitStack() as ls:
                stg = [T(ls, f"stg{i}", [128, D], F32) for i in range(4)]
                stg_b = [Buf(f"stg{i}") for i in range(4)]
                nt = N // 128
                grp = min(4, nt)
                bk = 0
                for g0 in range(0, nt, grp):
                    for j in range(grp):
                        S.dma("sp", stg[j][:], src_d[(g0 + j) * 128:(g0 + j + 1) * 128, :], writes=[stg_b[j]])
                    for fc in range(8):
                        bank = bk % 8
                        bk += 1
                        for j in range(grp):
                            S.op(pe, lambda e: e.transpose(ps[:, bank, j * 128:(j + 1) * 128],
                                                           stg[j][:, fc * 128:(fc + 1) * 128], ident[:]),
                                 reads=[stg_b[j], ident_b], writes=[psb[bank]], inc=(j == grp - 1))
                        CP(dve if fc % 2 == 0 else act, dst.t[:, fc, g0 * 128:(g0 + grp) * 128],
                           ps[:, bank, 0:grp * 128], [psb[bank]], dst.bufs(fc, fc + 1, g0 * 128, (g0 + grp) * 128))
                S.barrier()

        def store_seq(b):
            with ExitStack() as ss:
                sqt = T(ss, "f_sq", [128, 8, 512], BF16)
                tx = [T(ss, f"f_tx{i}", [128, 512], F32) for i in range(2)]
                rstd = [T(ss, "f_rstd", [128, 512], F32), T(ss, "f_rstd2", [128, 512], F32)]
                tb = Buf("f_tmp")
                yTs = [T(ss, f"f_yT{i}", [128, 8, 512], F32) for i in range(2)]
                y_bs = [[Buf(f"f_y{j}_{i}") for i in range(8)] for j in range(2)]
                stg = [T(ss, f"f_stg{i}", [128, D], F32) for i in range(2)]
                stg_b = [Buf(f"f_stg{i}") for i in range(2)]
                bk = 0
                si = 0
                for tt in range(NTOK // 512):
                    t0 = tt * 512
                    yT = yTs[tt % 2]
                    y_b = y_bs[tt % 2]
                    norm_mod(xT, t0, 512, lambda fc: yT[:, fc, :], lambda fc: [y_b[fc]],
                             lambda fc: gT[:, 4, fc:fc + 1], None, 7, (sqt, tx, rstd, tb))
                    for j in range(4):
                        sg = si % 2
                        si += 1
                        for half in range(2):
                            bank = bk % 6
                            bk += 1
                            for q in range(4):
                                fc = half * 4 + q
                                S.op(pe, lambda e: e.transpose(ps[:, bank, q * 128:(q + 1) * 128],
                                                               yT[:, fc, j * 128:(j + 1) * 128], ident[:]),
                                     reads=[y_b[fc], ident_b], writes=[psb[bank]], inc=(q == 3))
                            CP(dve if half == 0 else act, stg[sg][:, half * 512:(half + 1) * 512], ps[:, bank, :],
                               [psb[bank]], [stg_b[sg]])
                        S.dma("sp", out_d[b, t0 + j * 128:t0 + (j + 1) * 128, :], stg[sg][:],
                              reads=[stg_b[sg]], writes=[outB])
                S.barrier()

        def layer_pass(l, sidx, N, X: TT, is_ctx, kv_only):
            tw = min(512, N)
            ntt = N // tw
            n128 = N // 128
            nkt = 2 if is_ctx else 18
            koff = 0 if is_ctx else LCTX
            with ExitStack() as M:
                if is_ctx:
                    KT, VA = ctxK, ctxV
                    KT_b = [ctxK_b]
                    VA_b = [ctxV_b]
                    kbuf = lambda kt: ctxK_b
                    vbuf = lambda kt: ctxV_b
                else:
                    REG = T(M, "REG", [128, 15872], BF16)
                    KT = REG[:, 4096:4096 + 4608].rearrange("p (a b) -> p a b", a=2)
                    VA = REG[:, 8704:8704 + 6912].rearrange("p (a b) -> p a b", b=384)
                    KT_b = [Buf(f"KT{i}") for i in range(18)]
                    VA_b = [Buf(f"VA{i}") for i in range(18)]
                    kbuf = lambda kt: KT_b[kt]
                    vbuf = lambda kt: VA_b[kt]
                if is_ctx:
                    REG = T(M, "REGc", [128, 15872], BF16)
                if not kv_only:
                    QT_t = REG[:, 0:2 * N].rearrange("p (a b) -> p a b", a=2)
                    QT = TT("QT", QT_t, 2, N, tw)
                    mixAB_t = T(M, "mixAB", [128, 4, N], BF16)
                    mixAB = TT("mixAB", mixAB_t, 4, N, tw)
                    FT = T(M, "FT", [128, n128, 256], BF16)
                    FT_b = [Buf(f"FT{i}") for i in range(n128)]

                if is_ctx:
                    S.op(pool, lambda e: e.memset(ctxV[:, :, 64:128], 1.0), writes=[ctxV_b])
                    S.op(pool, lambda e: e.memset(ctxV[:, :, 256:320], 1.0), writes=[ctxV_b])

                with ExitStack() as P1:
                    hT_t = T(P1, "hT", [128, 8, N], BF16)
                    hT = TT("hT", hT_t, 8, N, tw)
                    cv_off = [0]

                    def carve(shape, dt):
                        n = 1
                        for d_ in shape[1:]:
                            n *= d_
                        nb = n * (4 if dt == F32 else 2)
                        o = cv_off[0]
                        cv_off[0] += (nb + 63) // 64 * 64
                        assert cv_off[0] <= 15872 * 2, cv_off[0]
                        ap = REG[:, o // 2:(o + nb) // 2]
                        if dt == F32:
                            ap = ap.bitcast(F32)
                        if len(shape) == 3:
                            ap = ap.rearrange("p (a b) -> p a b", a=shape[1])
                        return ap

                    if True:
                        sqt = carve([128, 8, tw], BF16)
                        tx = [carve([128, tw], F32) for i in range(2)]
                        rstd = [carve([128, tw], F32), T(P1, "n_rstd2", [128, tw], F32)]
                        tb = Buf("n_tmp")
                        for tt in range(ntt):
                            t0 = tt * tw
                            norm_mod(X, t0, tw, lambda fc: hT_t[:, fc, t0:t0 + tw],
                                     lambda fc: hT.bufs(fc, fc + 1, t0, t0 + tw),
                                     lambda fc: scg[:, l, 0, fc, sidx:sidx + 1], lambda fc: mod(l, 0, fc, sidx),
                                     tt % 2, (sqt, tx, rstd, tb))
                    dump(f"h_{l}_{sidx}", hT_t[:], hT.bufs(0, 8, 0, N))
                    chk("p1n_" + ("c" if is_ctx else "m"))

                    wsl = [T(P1, f"wsl{i}", [128, 8, 256], BF16) for i in range(3)]
                    wsl_b = [Buf(f"wsl{i}") for i in range(3)]
                    wcnt = [0]

                    def load_group(g):
                        sl = wcnt[0] % 3
                        wcnt[0] += 1
                        src = win_f[l].rearrange("(k p) c -> p k c", p=128)[:, :, g * 256:(g + 1) * 256]
                        wload_f("in", l, wsl[sl][:], wsl_b[sl], src)
                        return sl

                    halfc = [0]

                    def fm_chunk(sl, c):
                        hb = halfc[0] % 2
                        halfc[0] += 1
                        for tt in range(ntt):
                            bank = hb * 4 + tt
                            for k in range(8):
                                mm(ps[:, bank, 0:tw], wsl[sl][:, k, c * 128:(c + 1) * 128],
                                   hT_t[:, k, tt * tw:(tt + 1) * tw], k == 0, k == 7,
                                   [wsl_b[sl]] + hT.bufs(k, k + 1, tt * tw, (tt + 1) * tw), [psb[bank]])
                        return hb

                    def psv(hb):
                        return ps[:, hb * 4:hb * 4 + ntt, 0:tw], psb[hb * 4:hb * 4 + ntt]

                    def v3(ap2d):
                        return ap2d.rearrange("p (a b) -> p a b", b=tw)

                    if kv_only:
                        order = [7, 8]
                    else:
                        order = [0, 1, 2, 4, 3, 5, 6, 7, 8]
                    slots = {}
                    pre = order[:3]
                    for g in pre:
                        slots[g] = load_group(g)
                    nxt = [3]

                    def release_and_prefetch():
                        if nxt[0] < len(order):
                            g = order[nxt[0]]
                            nxt[0] += 1
                            slots[g] = load_group(g)

                    with ExitStack() as tmpm:
                        if not kv_only:
                            sl = slots[0]
                            for c in range(2):
                                hb = fm_chunk(sl, c)
                                pv, pb = psv(hb)
                                A(v3(mixAB_t[:, c, :]), pv, AF.Gelu_apprx_tanh, pb, mixAB.bufs(c, c + 1, 0, N))
                            release_and_prefetch()
                            chk("p1a_" + ("c" if is_ctx else "m"))
                            sl = slots[1]
                            NAV = 4
                            vt = [carve([128, 256], F32) for i in range(NAV)]
                            vs = [carve([128, 256], F32) for i in range(NAV)]
                            ss4 = [carve([128, 4], F32) for i in range(NAV)]
                            vnp = [carve([128, 4, 128], BF16) for i in range(NAV)]
                            gtm = [carve([128, 2, 128], F32) for i in range(NAV)]
                            av_b = [Buf(f"a_b{i}") for i in range(NAV)]
                            vnp_b = [Buf(f"a_vnp{i}") for i in range(NAV)]
                            gt_b = [Buf(f"a_gt{i}") for i in range(NAV)]
                            for i in range(NAV):
                                S.op(pool, lambda e: e.memset(vnp[i], 0.0), writes=[vnp_b[i]])
                            def av_s1(t):
                                i = t % NAV
                                bank = t % 4
                                for k in range(8):
                                    mm(ps[:, bank, 0:256], hT_t[:, k, t * 128:(t + 1) * 128], wsl[sl][:, k, :],
                                       k == 0, k == 7, [wsl_b[sl]] + hT.bufs(k, k + 1, t * 128, (t + 1) * 128),
                                       [psb[bank]])
                                A(vt[i], ps[:, bank, 0:256], AF.Gelu_apprx_tanh, [psb[bank]], [av_b[i]])
                                TTop(dve, vs[i], vt[i], vt[i], ALU.mult, [av_b[i]], [av_b[i]])
                                S.op(dve, lambda e: e.tensor_reduce(ss4[i], vs[i].rearrange("p (h d) -> p h d", d=64),
                                                                    AX.X, ALU.add), reads=[av_b[i]], writes=[av_b[i]])

                            def av_s2(t):
                                i = t % NAV
                                bank2 = 4 + t % 4
                                A(ss4[i], ss4[i], AF.Sqrt, [av_b[i], eps_b], [av_b[i]], bias=epsT[:, 0:1], scale=1.0 / 64)
                                S.op(dve, lambda e: e.reciprocal(ss4[i], ss4[i]), reads=[av_b[i]], writes=[av_b[i]])
                                v4 = vt[i].rearrange("p (h d) -> p h d", d=64)
                                TTop(dve, vnp[i][:, 0::2, 0:64], v4[:, 0::2, :],
                                     ss4[i][:, 0::2].unsqueeze(2).to_broadcast([128, 2, 64]), ALU.mult,
                                     [av_b[i]], [vnp_b[i]])
                                TTop(dve, vnp[i][:, 1::2, 64:128], v4[:, 1::2, :],
                                     ss4[i][:, 1::2].unsqueeze(2).to_broadcast([128, 2, 64]), ALU.mult,
                                     [av_b[i]], [vnp_b[i]])
                                for c in range(2):
                                    for hh in range(2):
                                        mm(ps[:, bank2, c * 128:(c + 1) * 128], vnp[i][:, 2 * c + hh, :],
                                           wsT[:, l, 2 * c + hh, :], hh == 0, hh == 1, [vnp_b[i], wsT_b], [psb[bank2]],
                                           inc=(hh == 1 and c == 1))

                            def av_s3(t):
                                i = t % NAV
                                bank2 = 4 + t % 4
                                TTop(dve, gtm[i], ps[:, bank2, 0:256].rearrange("p (c q) -> p c q", q=128),
                                     gbias[:, l, :, :], ALU.add, [psb[bank2], gbias_b], [gt_b[i]])
                                TTop(dve, mixAB_t[:, 0:2, t * 128:(t + 1) * 128], gtm[i],
                                     mixAB_t[:, 0:2, t * 128:(t + 1) * 128], ALU.mult,
                                     [gt_b[i]] + mixAB.bufs(0, 2, t * 128, (t + 1) * 128),
                                     mixAB.bufs(0, 2, t * 128, (t + 1) * 128))

                            blocks = [list(range(b0, min(b0 + NAV, n128))) for b0 in range(0, n128, NAV)]
                            for bi, blk in enumerate(blocks):
                                if bi == 0:
                                    for t in blk:
                                        av_s1(t)
                                for t in blk:
                                    av_s2(t)
                                if bi + 1 < len(blocks):
                                    for t in blocks[bi + 1]:
                                        av_s1(t)
                                for t in blk:
                                    av_s3(t)
                            release_and_prefetch()
                            chk("p1b_" + ("c" if is_ctx else "m"))
                            slx, slc, slb = slots[2], slots[4], slots[3]
                            S.barrier()
                            cv_off[0] = 0
                            Xb = carve([128, N], BF16); Xb_b = Buf("b_X")
                            cx = carve([128, N], F32); cx_b = Buf("b_cx")
                            yt = [carve([128, tw], F32) for i in range(2)]
                            yt_b = [Buf(f"b_y{i}") for i in range(2)]
                            yi = 0
                            for c in range(2):
                                hb = fm_chunk(slx, c)
                                pv, pb = psv(hb)
                                CP(act, v3(Xb), pv, pb, [Xb_b])
                                hb = fm_chunk(slc, c)
                                pv, pb = psv(hb)
                                TTop(dve, v3(cx), pv, v3(Xb), ALU.mult, pb + [Xb_b], [cx_b])
                                hb = fm_chunk(slb, c)
                                for tt in range(ntt):
                                    t0 = tt * tw
                                    y = yt[yi % 2]; yb = yt_b[yi % 2]; yi += 1
                                    A(y, cx[:, t0:t0 + tw], AF.Identity, [cx_b, sconv_b], [yb], scale=sconv[:, l, c, 1:2])
                                    lo = 1 if t0 == 0 else 0
                                    STT(dve, y[:, lo:tw], cx[:, t0 + lo - 1:t0 + tw - 1], sconv[:, l, c, 0:1], y[:, lo:tw],
                                        ALU.mult, ALU.add, [cx_b, sconv_b, yb], [yb])
                                    hi = tw - 1 if t0 + tw == N else tw
                                    STT(dve, y[:, 0:hi], cx[:, t0 + 1:t0 + hi + 1], sconv[:, l, c, 2:3], y[:, 0:hi],
                                        ALU.mult, ALU.add, [cx_b, sconv_b, yb], [yb])
                                    bank = hb * 4 + tt
                                    TTop(dve, mixAB_t[:, 2 + c, t0:t0 + tw], ps[:, bank, 0:tw], y, ALU.mult,
                                         [psb[bank], yb], mixAB.bufs(2 + c, 3 + c, t0, t0 + tw))
                            release_and_prefetch()
                            release_and_prefetch()
                            release_and_prefetch()
                            chk("p1c_" + ("c" if is_ctx else "m"))
                            sl = slots[5]
                            for t in range(n128):
                                bank = t % 8
                                for k in range(8):
                                    mm(ps[:, bank, 0:256], hT_t[:, k, t * 128:(t + 1) * 128], wsl[sl][:, k, :],
                                       k == 0, k == 7, [wsl_b[sl]] + hT.bufs(k, k + 1, t * 128, (t + 1) * 128),
                                       [psb[bank]])
                                CP(act if t % 2 == 0 else dve, FT[:, t, :], ps[:, bank, 0:256], [psb[bank]], [FT_b[t]])
                            release_and_prefetch()

                        chk("p1d_" + ("c" if is_ctx else "m"))
                        S.barrier()
                        if not is_ctx:
                            S.op(pool, lambda e: e.memset(VA[:, :, 64:128], 1.0), writes=VA_b)
                            S.op(pool, lambda e: e.memset(VA[:, :, 256:320], 1.0), writes=VA_b)
                            CP(pool, KT[:, :, 0:LCTX], ctxK[:], [ctxK_b], KT_b[0:2])
                            CP(pool, VA[:, 0:2, :], ctxV[:], [ctxV_b], VA_b[0:2])
                        chk("p1d1_" + ("c" if is_ctx else "m"))
                        zb = T(tmpm, "r_zb", [128, N], BF16); zb_b = Buf("r_zb")
                        t1 = [T(tmpm, f"r_t1{i}", [128, tw], F32) for i in range(1)]
                        t2 = [T(tmpm, f"r_t2{i}", [128, tw], F32) for i in range(1)]
                        rt_b = [Buf(f"r_t{i}") for i in range(1)]
                        ri = 0
                        for g in ([7] if kv_only else [6, 7]):
                            sl = slots[g]
                            for c in range(2):
                                hb = fm_chunk(sl, c)
                                pv, pb = psv(hb)
                                if g == 6:
                                    dst2d = QT_t[:, c, :]
                                    dstb = lambda a, b_: QT.bufs(c, c + 1, a, b_)
                                else:
                                    dst2d = KT[:, c, koff:koff + N]
                                    if is_ctx:
                                        dstb = lambda a, b_: [ctxK_b]
                                    else:
                                        dstb = lambda a, b_: KT_b[(koff + a) // 128:(koff + b_ - 1) // 128 + 1]
                                if is_ctx:
                                    CP(act, v3(dst2d), pv, pb, dstb(0, N))
                                    continue
                                CP(act, v3(zb[:]), pv, pb, [zb_b])
                                chk("p1d2_" + ("c" if is_ctx else "m"))
                                hb2 = 1 - hb
                                halfc[0] += 1
                                for tt in range(ntt):
                                    mm(ps[:, hb2 * 4 + tt, 0:tw], rperm[:], zb[:, tt * tw:(tt + 1) * tw], True, True,
                                       [rperm_b, zb_b], [psb[hb2 * 4 + tt]])
                                chk("p1d3_" + ("c" if is_ctx else "m"))
                                for tt in range(ntt):
                                    t0 = tt * tw
                                    i = 0; ri += 1
                                    TTop(dve, t1[i][:], ps[:, hb * 4 + tt, 0:tw], rope[:, 0, t0:t0 + tw], ALU.mult,
                                         [psb[hb * 4 + tt], rope_b], [rt_b[i]])
                                    TTop(dve, t2[i][:], ps[:, hb2 * 4 + tt, 0:tw], rope[:, 1, t0:t0 + tw], ALU.mult,
                                         [psb[hb2 * 4 + tt], rope_b], [rt_b[i]])
                                    TTop(dve, dst2d[:, t0:t0 + tw], t1[i][:], t2[i][:], ALU.add, [rt_b[i]], dstb(t0, t0 + tw))
                            release_and_prefetch()
                        chk("p1e_" + ("c" if is_ctx else "m"))
                        sl = slots[8]
                        for t in range(n128):
                            bank = t % 8
                            kt = koff // 128 + t
                            for k in range(8):
                                mm(ps[:, bank, 0:256], hT_t[:, k, t * 128:(t + 1) * 128], wsl[sl][:, k, :],
                                   k == 0, k == 7, [wsl_b[sl]] + hT.bufs(k, k + 1, t * 128, (t + 1) * 128), [psb[bank]])
                            src4 = ps[:, bank, 0:256].rearrange("p (h d) -> p h d", d=64)
                            dst6 = VA[:, kt, :].rearrange("p (b d) -> p b d", d=64)
                            CP(act, dst6[:, 0:3:2, :], src4[:, 0:2, :], [psb[bank]], [vbuf(kt)])
                            CP(dve, dst6[:, 3:6:2, :], src4[:, 2:4, :], [psb[bank]], [vbuf(kt)])
                        S.barrier()
                if kv_only:
                    return
                chk("p1_" + ("c" if is_ctx else "m"))
                dump(f"mixAB_{l}_{sidx}", mixAB_t[:], mixAB.bufs(0, 4, 0, N))
                dump(f"QT_{l}_{sidx}", QT_t[:], QT.bufs(0, 2, 0, N))

                with ExitStack() as P23:
                    mixCD_t = T(P23, "mixCD", [128, 4, N], BF16)
                    mixCD = TT("mixCD", mixCD_t, 4, N, tw)
                    with ExitStack() as P2:
                        ngr = n128 // 4 if not is_ctx else 1
                        gsz = 4 if not is_ctx else 2
                        kw_ = tw
                        P2d = ExitStack()
                        dtile = [T(P2d, f"dft{i}", [128, gsz, 2, kw_], BF16) for i in range(3)]
                        dt_b = [Buf(f"dft{i}") for i in range(3)]
                        U = [T(P2d, f"dU{i}", [128, 4, kw_], BF16) for i in range(2)]
                        U_b = [Buf(f"dU{i}") for i in range(2)]
                        di = 0
                        for ktile in range(ntt):
                            for gi in range(ngr):
                                sl = di % 3; di += 1
                                if is_ctx:
                                    S.dma("sp", dtile[sl][:], cn256_d, writes=[dt_b[sl]])
                                else:
                                    S.dma("sp", dtile[sl][:], cn_d[ktile, :, gi * 4:(gi + 1) * 4, :, :], writes=[dt_b[sl]])
                                for j in range(gsz):
                                    ntile = gi * gsz + j
                                    first = (ntile == 0)
                                    last = (ntile == n128 - 1)
                                    for cs in range(2):
                                        for c in range(2):
                                            bank = cs * 2 + c
                                            mm(ps[:, bank, 0:kw_], FT[:, ntile, c * 128:(c + 1) * 128], dtile[sl][:, j, cs, :],
                                               first, last, [FT_b[ntile], dt_b[sl]], [psb[bank]],
                                               inc=(last and cs == 1 and c == 1) or (j == gsz - 1 and cs == 1 and c == 1))
                            ui = ktile % 2
                            CP(act, U[ui][:, 0:2, :], ps[:, 0:2, 0:kw_], psb[0:2], [U_b[ui]])
                            CP(dve, U[ui][:, 2:4, :], ps[:, 2:4, 0:kw_], psb[2:4], [U_b[ui]])
                            for c in range(2):
                                bank = 4 + (2 * ktile + c) % 4
                                mm(ps[:, bank, 0:kw_], c64[:, 0, :], U[ui][:, c, :], True, False, [c64_b, U_b[ui]], [psb[bank]])
                                mm(ps[:, bank, 0:kw_], c64[:, 1, :], U[ui][:, 2 + c, :], False, True, [c64_b, U_b[ui]], [psb[bank]])
                                CP(act if c == 0 else dve, mixCD_t[:, c, ktile * kw_:(ktile + 1) * kw_], ps[:, bank, 0:kw_],
                                   [psb[bank]], mixCD.bufs(c, c + 1, ktile * kw_, (ktile + 1) * kw_))
                        S.barrier()
                        P2d.close()
                        chk("p2d_" + ("c" if is_ctx else "m"))

                        PTW = [T(P2, f"PT{i}", [128, 2, tw], BF16) for i in range(4)]
                        PT_b = [Buf(f"PT{i}") for i in range(4)]
                        osum = T(P2, "osum", [128, tw], F32); osum_b = Buf("osum")
                        osq = T(P2, "osq", [128, tw], BF16); osq_b = Buf("osq")
                        orst = T(P2, "orst", [128, tw], F32); orst_b = Buf("orst")
                        scl = 32.0 ** -0.5
                        RP = [(0, 1), (2, 3), (6, 7)]
                        obank = [4, 5]
                        steps = [(h, qt, kt) for h in range(4) for qt in range(ntt) for kt in range(nkt)]
                        NS = len(steps)
                        LOOK = 2

                        def hinfo(h):
                            odd = h % 2
                            if odd:
                                vsl = slice(64 + (h // 2) * 192, 64 + (h // 2) * 192 + 128)
                            else:
                                vsl = slice((h // 2) * 192, (h // 2) * 192 + 128)
                            return h // 2, odd, (64 if odd else 0), (0 if odd else 64), vsl

                        def issue_qk_exp(si):
                            h, qt, kt = steps[si]
                            c, odd, pr0, ps_row, vsl = hinfo(h)
                            q0 = qt * tw
                            pair = RP[si % 3]
                            if WARM and not is_ctx:
                                mm(ps[:, pair[0], 0:tw], zfill[:, 0:128], zfill[:, 0:tw], True, True,
                                   [zfill_b], [psb[pair[0]]])
                            for m in range(2):
                                pb = odd * 64 + m * 32
                                mm(ps[:, pair[m], 0:tw], KT[pb:pb + 32, c, kt * 128:(kt + 1) * 128],
                                   QT_t[pb:pb + 32, c, q0:q0 + tw], True, True,
                                   [kbuf(kt)] + QT.bufs(c, c + 1, q0, q0 + tw), [psb[pair[m]]], tp=(pb, 0))
                            i = si % 4
                            A(PTW[i][:, :, 0:tw], ps[:, pair[0]:pair[0] + 2, 0:tw], AF.Exp,
                              [psb[pair[0]], psb[pair[1]]], [PT_b[i]], scale=scl)

                        def issue_av(si):
                            h, qt, kt = steps[si]
                            c, odd, pr0, ps_row, vsl = hinfo(h)
                            i = si % 4
                            for m in range(2):
                                mm(ps[:, obank[m], 0:tw], VA[:, kt, vsl], PTW[i][:, m, 0:tw], kt == 0, kt == nkt - 1,
                                   [vbuf(kt), PT_b[i]], [psb[obank[m]]])

                        osb2 = T(P2, "osb2", [128, 2, tw], F32); osb2_b = Buf("osb2")
                        rrow2 = T(P2, "rrow2", [128, 2, tw], F32); rrow2_b = Buf("rrow2")
                        bcT2 = T(P2, "bcT2", [128, 2, tw], F32); bcT2_b = Buf("bcT2")

                        def epi_a(si):
                            h, qt, kt = steps[si]
                            c, odd, pr0, ps_row, vsl = hinfo(h)
                            pr = slice(pr0, pr0 + 64)
                            rw = slice(ps_row, ps_row + 1)
                            CP(dve, osb2[:, :, 0:tw], ps[:, obank[0]:obank[0] + 2, 0:tw], [psb[obank[0]], psb[obank[1]]], [osb2_b])
                            A(rrow2[rw, :, 0:tw], osb2[rw, :, 0:tw], AF.Ln, [osb2_b], [rrow2_b])
                            A(rrow2[rw, :, 0:tw], rrow2[rw, :, 0:tw], AF.Exp, [rrow2_b], [rrow2_b], scale=-1.0)
                            S.dma("sp", scr_d[:, 0:tw].unsqueeze(0), rrow2[rw, :, 0:tw], reads=[rrow2_b], writes=[scr_b[0]])
                            S.dma("sp", bcT2[pr, :, 0:tw], scr_d[:, 0:tw].unsqueeze(0).to_broadcast([64, 2, tw]),
                                  reads=[scr_b[0]], writes=[bcT2_b])

                        def epi_b(si, sj):
                            h, qt, kt = steps[si]
                            c, odd, pr0, ps_row, vsl = hinfo(h)
                            q0 = qt * tw
                            pr = slice(pr0, pr0 + 64)
                            TTop(dve, osb2[pr, 0, 0:tw], osb2[pr, 0, 0:tw], bcT2[pr, 0, 0:tw], ALU.mult,
                                 [osb2_b, bcT2_b], [osb2_b])
                            STT(dve, osb2[pr, 1, 0:tw], osb2[pr, 1, 0:tw], neglam[pr, l:l + 1], bcT2[pr, 1, 0:tw],
                                ALU.mult, ALU.mult, [osb2_b, bcT2_b, coef_b], [osb2_b])
                            TTop(dve, osum[pr, 0:tw], osb2[pr, 0, 0:tw], osb2[pr, 1, 0:tw], ALU.add,
                                 [osb2_b], [osum_b])
                            TTop(dve, osq[pr, 0:tw], osum[pr, 0:tw], osum[pr, 0:tw], ALU.mult, [osum_b], [osq_b])
                            bank = RP[sj % 3][0]
                            mm(ps[:, bank, 0:tw], ones_bf[pr, :], osq[pr, 0:tw], True, True, [ones_b, osq_b], [psb[bank]])
                            A(orst[pr, 0:tw], ps[pr, bank, 0:tw], AF.Ln, [psb[bank], eps_b], [orst_b],
                              bias=epsT[pr, 0:1], scale=1.0 / 64)
                            A(orst[pr, 0:tw], orst[pr, 0:tw], AF.Exp, [orst_b], [orst_b], scale=-0.5)
                            STT(dve, mixCD_t[pr, 2 + c, q0:q0 + tw], osum[pr, 0:tw], subln[pr, l:l + 1], orst[pr, 0:tw],
                                ALU.mult, ALU.mult, [osum_b, orst_b, subln_b], mixCD.bufs(2 + c, 3 + c, q0, q0 + tw))

                        LAG = min(12, nkt - 1)
                        pend = {}
                        for si in range(min(LOOK, NS)):
                            issue_qk_exp(si)
                        for si in range(NS):
                            if si + LOOK < NS:
                                issue_qk_exp(si + LOOK)
                            issue_av(si)
                            if si in pend:
                                epi_b(pend.pop(si), si)
                            if steps[si][2] == nkt - 1:
                                epi_a(si)
                                if si + LAG < NS:
                                    pend[si + LAG] = si
                                else:
                                    epi_b(si, si)
                        S.barrier()
                    dump(f"mixCD_{l}_{sidx}", mixCD_t[:], mixCD.bufs(0, 4, 0, N))
                    chk("p2_" + ("c" if is_ctx else "m"))

                    with ExitStack() as P3:
                        wo = [T(P3, f"wo{i}", [128, 8, 256], BF16) for i in range(2)]
                        wo_b = [Buf(f"wo{i}") for i in range(2)]
                        bk = 0
                        for fo in range(8):
                            sl = (fo // 2) % 2
                            if fo % 2 == 0:
                                src = wout_f[l].rearrange("(k p) c -> p k c", p=128)[:, :, fo * 128:(fo + 2) * 128]
                                wload_f("out", l, wo[sl][:], wo_b[sl], src)
                            wcol = (fo % 2) * 128
                            for tt in range(ntt):
                                t0 = tt * tw
                                bank = bk % 8; bk += 1
                                for k in range(8):
                                    if k < 4:
                                        rhs = mixAB_t[:, k, t0:t0 + tw]; rb = mixAB.bufs(k, k + 1, t0, t0 + tw)
                                    else:
                                        rhs = mixCD_t[:, k - 4, t0:t0 + tw]; rb = mixCD.bufs(k - 4, k - 3, t0, t0 + tw)
                                    mm(ps[:, bank, 0:tw], wo[sl][:, k, wcol:wcol + 128], rhs, k == 0, k == 7, [wo_b[sl]] + rb, [psb[bank]])
                                xb_ = X.bufs(fo, fo + 1, t0, t0 + tw)
                                STT(dve, X.t[:, fo, t0:t0 + tw], ps[:, bank, 0:tw], mod(l, 2, fo, sidx), X.t[:, fo, t0:t0 + tw],
                                    ALU.mult, ALU.add, [psb[bank], mod_b] + xb_, xb_)
                        S.barrier()
            dump(f"xmid_{l}_{sidx}", X.t[:], X.bufs(0, 8, 0, N))
            chk("p3_" + ("c" if is_ctx else "m"))

            with ExitStack() as P4:
                stw = min(1024, N)
                nst = N // stw
                nsub = stw // tw
                g_t = T(P4, "g_t", [128, NFF, stw], BF16)
                g_b = [[Buf(f"g{j}_{s}") for s in range(nsub)] for j in range(NFF)]
                h2 = T(P4, "h2", [128, 8, stw + 1], BF16)
                h2_b = [Buf(f"h2_{s}") for s in range(nsub + 1)]
                sqt = T(P4, "n_sq", [128, 8, tw], BF16)
                tx = [T(P4, f"n_tx{i}", [128, tw], F32) for i in range(2)]
                rstd = T(P4, "n_rstd", [128, tw], F32)
                tb = Buf("n_tmp")
                wu = [T(P4, f"wu{i}", [128, 8, 512], BF16) for i in range(2)]
                wu_b = [Buf(f"wu{i}") for i in range(2)]
                wd = [T(P4, f"wd{i}", [128, NFF, 128], BF16) for i in range(2)]
                wd_b = [Buf(f"wd{i}") for i in range(2)]
                yv = [T(P4, f"f_y{i}", [128, stw], F32) for i in range(2)]
                yv_b = [Buf(f"f_y{i}") for i in range(2)]
                HB = 7
                wui = 0
                wdi = 0
                h2h = T(P4, "h2h", [128, 8, 2], BF16)
                h2h_b = Buf("h2h")
                if nst > 1:
                    for sti in range(nst):
                        hcol = sti * stw + stw if sti == 0 else sti * stw - 1
                        norm_mod(X, hcol, 1, lambda fc: h2h[:, fc, sti:sti + 1], lambda fc: [h2h_b],
                                 lambda fc: scg[:, l, 1, fc, sidx:sidx + 1], lambda fc: mod(l, 3, fc, sidx),
                                 2 + sti, (sqt, tx, rstd, tb))
                for sti in range(nst):
                    s0 = sti * stw
                    halo = None
                    if nst > 1:
                        halo = s0 + stw if sti == 0 else s0 - 1
                    def h2_gens(s0_):
                        gs = []
                        for sb_ in range(nsub):
                            gs.append(norm_mod_g(X, s0_ + sb_ * tw, tw,
                                                 (lambda sb__: (lambda fc: h2[:, fc, sb__ * tw:(sb__ + 1) * tw]))(sb_),
                                                 (lambda sb__: (lambda fc: [h2_b[sb__]]))(sb_),
                                                 lambda fc: scg[:, l, 1, fc, sidx:sidx + 1], lambda fc: mod(l, 3, fc, sidx),
                                                 7, (sqt, tx, rstd, tb)))
                        return gs
                    if sti == 0:
                        for g_ in h2_gens(s0):
                            for _ in g_:
                                pass
                    bring = 0
                    for j in range(NFF):
                        if j % 2 == 0:
                            wui += 1
                            sl = wui % 2
                            srcw = wup_f[l].rearrange("(k p) c -> p k c", p=128)
                            wload_f("up", l, wu[sl][:, :, 0:256], wu_b[sl], srcw[:, :, j * 128:(j + 2) * 128])
                            wload_f("up", l, wu[sl][:, :, 256:512], wu_b[sl], srcw[:, :, DFF + j * 128:DFF + (j + 2) * 128])
                        sl = wui % 2
                        ac0 = (j % 2) * 128
                        bc0 = 256 + (j % 2) * 128
                        ab = (j % 2) * 2
                        abanks = [ab + s for s in range(nsub)]
                        for sb in range(nsub):
                            for k in range(8):
                                mm(ps[:, abanks[sb], 0:tw], wu[sl][:, k, ac0:ac0 + 128], h2[:, k, sb * tw:(sb + 1) * tw],
                                   k == 0, k == 7, [wu_b[sl], h2_b[sb]], [psb[abanks[sb]]])
                        if halo is not None:
                            for k in range(8):
                                mm(ps[:, HB, j:j + 1], wu[sl][:, k, ac0:ac0 + 128], h2h[:, k, sti:sti + 1], k == 0, k == 7,
                                   [wu_b[sl], h2h_b], [psb[HB]])
                        bbanks = []
                        for sb in range(nsub):
                            bb = 4 + bring % 3; bring += 1
                            bbanks.append(bb)
                            for k in range(8):
                                mm(ps[:, bb, 0:tw], wu[sl][:, k, bc0:bc0 + 128], h2[:, k, sb * tw:(sb + 1) * tw],
                                   k == 0, k == 7, [wu_b[sl], h2_b[sb]], [psb[bb]])
                        i = j % 2
                        y = yv[i]; yb = yv_b[i]
                        apv = ps[:, ab:ab + nsub, 0:tw]
                        apb = [psb[b_] for b_ in abanks]
                        a2d = lambda lo_, hi_: None
                        y3 = y[:, 0:stw].rearrange("p (a b) -> p a b", b=tw)
                        A(y3, apv, AF.Identity, apb + [fconv_b], [yb], scale=fconv[:, l, j, 1:2])
                        for sb in range(nsub):
                            lo = 1 if sb == 0 else 0
                            if sb == 0:
                                STT(dve, y[:, 1:tw], ps[:, abanks[0], 0:tw - 1], fconv[:, l, j, 0:1], y[:, 1:tw],
                                    ALU.mult, ALU.add, [psb[abanks[0]], yb, fconv_b], [yb])
                            else:
                                STT(dve, y[:, sb * tw + 1:(sb + 1) * tw], ps[:, abanks[sb], 0:tw - 1], fconv[:, l, j, 0:1],
                                    y[:, sb * tw + 1:(sb + 1) * tw], ALU.mult, ALU.add, [psb[abanks[sb]], yb, fconv_b], [yb])
                                STT(dve, y[:, sb * tw:sb * tw + 1], ps[:, abanks[sb - 1], tw - 1:tw], fconv[:, l, j, 0:1],
                                    y[:, sb * tw:sb * tw + 1], ALU.mult, ALU.add, [psb[abanks[sb - 1]], yb, fconv_b], [yb])
                            STT(dve, y[:, sb * tw:(sb + 1) * tw - 1], ps[:, abanks[sb], 1:tw], fconv[:, l, j, 2:3],
                                y[:, sb * tw:(sb + 1) * tw - 1], ALU.mult, ALU.add, [psb[abanks[sb]], yb, fconv_b], [yb])
                            if sb + 1 < nsub:
                                STT(dve, y[:, (sb + 1) * tw - 1:(sb + 1) * tw], ps[:, abanks[sb + 1], 0:1], fconv[:, l, j, 2:3],
                                    y[:, (sb + 1) * tw - 1:(sb + 1) * tw], ALU.mult, ALU.add,
                                    [psb[abanks[sb + 1]], yb, fconv_b], [yb])
                        if halo is not None:
                            if sti == 0:
                                STT(dve, y[:, stw - 1:stw], ps[:, HB, j:j + 1], fconv[:, l, j, 2:3], y[:, stw - 1:stw],
                                    ALU.mult, ALU.add, [psb[HB], yb, fconv_b], [yb])
                            else:
                                STT(dve, y[:, 0:1], ps[:, HB, j:j + 1], fconv[:, l, j, 0:1], y[:, 0:1],
                                    ALU.mult, ALU.add, [psb[HB], yb, fconv_b], [yb])
                        A(y[:, 0:stw], y[:, 0:stw], AF.Silu, [yb], [yb])
                        for sb in range(nsub):
                            TTop(dve, g_t[:, j, sb * tw:(sb + 1) * tw], ps[:, bbanks[sb], 0:tw], y[:, sb * tw:(sb + 1) * tw],
                                 ALU.mult, [psb[bbanks[sb]], yb], [g_b[j][sb]])
                    bk = 0
                    nxt_gens = h2_gens(s0 + stw) if sti + 1 < nst else []
                    def advance(n):
                        for _ in range(n):
                            while nxt_gens:
                                try:
                                    next(nxt_gens[0])
                                    break
                                except StopIteration:
                                    nxt_gens.pop(0)
                    for fo in range(8):
                        sl = wdi % 2; wdi += 1
                        srcd = wdown_d[l].rearrange("(j p) c -> p j c", p=128)[:, :, fo * 128:(fo + 1) * 128]
                        wload("down", l, fo, wd[sl][:], wd_b[sl], wdown_s[l, fo], [(wd[sl][:], srcd)])
                        for sb in range(nsub):
                            bank = bk % 7; bk += 1
                            for j in range(NFF):
                                mm(ps[:, bank, 0:tw], wd[sl][:, j, :], g_t[:, j, sb * tw:(sb + 1) * tw], j == 0, j == NFF - 1,
                                   [wd_b[sl], g_b[j][sb]], [psb[bank]])
                            t0 = s0 + sb * tw
                            xb_ = X.bufs(fo, fo + 1, t0, t0 + tw)
                            STT(dve, X.t[:, fo, t0:t0 + tw], ps[:, bank, 0:tw], mod(l, 5, fo, sidx), X.t[:, fo, t0:t0 + tw],
                                ALU.mult, ALU.add, [psb[bank], mod_b] + xb_, xb_)
                            advance(2)
                    advance(1000)
                    S.barrier()
            dump(f"x_{l}_{sidx}", X.t[:], X.bufs(0, 8, 0, N))

        if True:
            chk("pro")
            for b in range(nseq):
                load_seq(x_d[b], NTOK, xT)
                load_seq(ctx_d[b], LCTX, xcT)
                chk("load")
                for l in range(nlayers):
                    layer_pass(l, 2, LCTX, xcT, True, kv_only=(l == 1))
                    chk("ctx")
                    if b == 0 and l == 0 and nlayers > 1:
                        issue_conversions(1)
                    layer_pass(l, b, NTOK, xT, False, kv_only=False)
                store_seq(b)
        S.halt = False
        for e_ in S.pending:
            S.pending[e_] = []
        S.barrier()
        S.final_wait("sp", [outB, dbgB])
        build_program.stats = dict(S.ninst)
        build_program.nsem = S.nsem
    return nc


def _fm(v):
    v = np.asarray(v, np.float32)
    k = v.shape[-1] // 128
    r = v.reshape(v.shape[:-1] + (k, 128))
    return np.ascontiguousarray(np.moveaxis(r, -1, 0))


def make_in_maps(inputs, nseq=NB_PER_CORE, ncores=N_CORES):
    C = _get_consts()
    f32 = lambda a: np.ascontiguousarray(np.asarray(a, np.float32))
    shared = {
        "w_ada": f32(inputs["w_ada"]), "w_in": f32(inputs["w_in"]), "w_out": f32(inputs["w_out"]),
        "w_up": f32(inputs["ffn_w_up"]), "w_down": f32(inputs["ffn_w_down"]),
        "b_adaT": _fm(inputs["b_ada"]),
        "gT": _fm(np.concatenate([f32(inputs["norm1_g"]), f32(inputs["norm2_g"]), f32(inputs["final_g"])[None]], 0)),
        "wsT": np.ascontiguousarray(f32(inputs["gmlp_ws"]).transpose(3, 0, 1, 2)),
        "gbiasT": np.ascontiguousarray(
            np.repeat(f32(inputs["gmlp_bs"]).reshape(2, 2, 2, 1, 128), 64, axis=3).reshape(2, 2, 128, 128).transpose(2, 0, 1, 3)),
        "sconvT": np.ascontiguousarray(f32(inputs["sconv_w"]).reshape(2, 3, 2, 128).transpose(3, 0, 2, 1)),
        "fconvT": np.ascontiguousarray(f32(inputs["ffn_conv_w"]).reshape(2, 3, NFF, 128).transpose(3, 0, 2, 1)),
        "sublnT": np.ascontiguousarray(np.tile(f32(inputs["subln_g"]), (1, 2)).T),
        "lamv": np.ascontiguousarray(np.broadcast_to(
            np.stack([f32(inputs["lambda_q1"]), f32(inputs["lambda_k1"]), f32(inputs["lambda_q2"]), f32(inputs["lambda_k2"])], 1)[None],
            (128, 2, 4, 32))),
        "ident": C["ident"], "c64": C["c64"], "rperm": C["rperm"], "rope": C["rope"], "cn": C["cn"], "cn256": C["cn256"],
    }
    x = f32(inputs["x"]); ctx = f32(inputs["ctx"]); c = f32(inputs["c"]); cc = f32(inputs["c_ctx"])
    maps = []
    for i in range(ncores):
        m = dict(shared)
        m["x"] = x[i * nseq:(i + 1) * nseq]
        m["ctx"] = ctx[i * nseq:(i + 1) * nseq]
        cv = np.stack([c[i * nseq + (s % nseq)] for s in range(2)] + [cc], 0)
        m["cT"] = np.ascontiguousarray(cv.reshape(3, 8, 128).transpose(2, 1, 0))
        maps.append(m)
    return maps


_NC_CACHE = {}


def kernel(**inputs):
    nc = build_program()
    maps = make_in_maps(inputs)
    res = run_bass_kernel_spmd(nc, maps, core_ids=list(range(N_CORES)))
    out = np.concatenate([np.asarray(r["out"], np.float32) for r in res.results], axis=0)
    return out
```

```python
import math
from contextlib import ExitStack
import numpy as np
import ml_dtypes
import concourse.bass as bass
import concourse.mybir as mybir
from concourse.bass_utils import run_bass_kernel_spmd

F32 = mybir.dt.float32
BF16 = mybir.dt.bfloat16
AF = mybir.ActivationFunctionType
ALU = mybir.AluOpType
AX = mybir.AxisListType

D = 1024
NTOK = 2048
LCTX = 256
DFF = 2816
NFF = DFF // 128
EPS = 1e-6
NB_PER_CORE = 2
N_CORES = 8
WARM = True


class Buf:
    __slots__ = ("name", "writers", "readers", "dsem")

    def __init__(self, name):
        self.name = name
        self.writers = {}
        self.readers = {}
        self.dsem = None


class Sync:
    EPOCH = 30000

    def __init__(self, nc, stack):
        self.nc = nc
        self.stack = stack
        self.eng = {"pe": nc.tensor, "act": nc.scalar, "dve": nc.vector,
                    "pool": nc.gpsimd, "sp": nc.sync}
        self.sems = {}
        self.total = {}
        self.cur = {}
        self.known = {e: {} for e in self.eng}
        self.pending = {e: [] for e in self.eng}
        self.nsem = 0
        for e in self.eng:
            self._new_engine_sem(e)
        self.ninst = {e: 0 for e in self.eng}
        self.halt = False

    def _alloc(self, name):
        h = self.stack.enter_context(self.nc.semaphore(name))
        self.nsem += 1
        return h

    def _new_engine_sem(self, e):
        key = ("E", e, self.nsem)
        self.sems[key] = self._alloc(f"s_{e}_{self.nsem}")
        self.total[key] = 0
        self.cur[e] = key

    def dma_sem(self, buf):
        if buf.dsem is None:
            key = ("D", buf.name)
            if key not in self.sems:
                self.sems[key] = self._alloc(f"d_{self.nsem}")
                self.total[key] = 0
            buf.dsem = key
        return buf.dsem

    def share_dsem(self, bufs):
        k = self.dma_sem(bufs[0])
        for b in bufs[1:]:
            b.dsem = k

    def _wait(self, e, need):
        engine = self.eng[e]
        for key, val in need.items():
            if key[0] == "D":
                val = self.total[key]
            if self.known[e].get(key, 0) >= val:
                continue
            engine.wait_ge(self.sems[key], val)
            self.known[e][key] = val

    def _collect(self, e, reads, writes):
        need = {}
        for b in reads:
            for k, v in b.writers.items():
                if need.get(k, 0) < v:
                    need[k] = v
        for b in writes:
            for k, v in list(b.writers.items()) + list(b.readers.items()):
                if k[0] == "E" and k[1] == e and e != "pool":
                    continue
                if need.get(k, 0) < v:
                    need[k] = v
        return need

    def _register(self, tok, reads, writes):
        k, v = tok
        for b in reads:
            if b.readers.get(k, 0) < v:
                b.readers[k] = v
        for b in writes:
            b.writers = {k: v}
            b.readers = {}

    def op(self, e, fn, reads=(), writes=(), inc=True):
        if self.halt:
            return None
        reads = list(reads)
        writes = list(writes)
        self._wait(e, self._collect(e, reads, writes))
        inst = fn(self.eng[e])
        self.ninst[e] += 1
        if not inc:
            self.pending[e].append((reads, writes))
            return inst
        key = self.cur[e]
        self.total[key] += 1
        inst.then_inc(self.sems[key], 1)
        tok = (key, self.total[key])
        for (r, w) in self.pending[e]:
            self._register(tok, r, w)
        self.pending[e] = []
        self._register(tok, reads, writes)
        if self.total[key] >= self.EPOCH:
            self._new_engine_sem(e)
        return inst

    def dma(self, e, out, in_, reads=(), writes=(), **kw):
        if self.halt:
            return None
        reads = list(reads)
        writes = list(writes)
        assert not self.pending[e]
        self._wait(e, self._collect(e, reads, writes))
        key = self.dma_sem(writes[0])
        inst = self.eng[e].dma_start(out=out, in_=in_, **kw)
        inst.then_inc(self.sems[key], 16)
        self.total[key] += 16
        self.ninst[e] += 1
        self._register((key, self.total[key]), reads, writes)
        return inst

    def barrier(self):
        if self.halt:
            return
        for e in self.eng:
            assert not self.pending[e], e
        need = {k: v for k, v in self.total.items() if v > 0}
        for e in self.eng:
            self._wait(e, dict(need))

    def final_wait(self, e, bufs):
        need = {}
        for b in bufs:
            for k, v in b.writers.items():
                need[k] = max(need.get(k, 0), v)
        self._wait(e, need)


class TT:
    def __init__(self, name, t, nchunk, ntok, blk):
        self.t = t
        self.nchunk = nchunk
        self.ntok = ntok
        self.blk = blk
        self.nblk = (ntok + blk - 1) // blk
        self.b = [[Buf(f"{name}_{c}_{j}") for j in range(self.nblk)] for c in range(nchunk)]

    def bufs(self, c0, c1, t0, t1):
        j0 = t0 // self.blk
        j1 = (t1 - 1) // self.blk + 1
        return [self.b[c][j] for c in range(c0, c1) for j in range(j0, j1)]


def _consts():
    bf = ml_dtypes.bfloat16
    c = {}
    c["ident"] = np.eye(128, dtype=np.float32)
    d = np.arange(64)
    ang = 2 * np.pi * ((d[:, None] * d[None, :]) % 64) / 64.0
    c64 = np.zeros((128, 2, 128), np.float64)
    for hh in range(2):
        c64[hh * 64:(hh + 1) * 64, 0, hh * 64:(hh + 1) * 64] = np.cos(ang)
        c64[hh * 64:(hh + 1) * 64, 1, hh * 64:(hh + 1) * 64] = -np.sin(ang)
    c["c64"] = c64.astype(bf)
    rp = np.zeros((128, 128), np.float32)
    for cp in range(128):
        j = cp % 16
        partner = cp + 8 if j < 8 else cp - 8
        rp[partner, cp] = 1.0
    c["rperm"] = rp.astype(bf)
    n = np.arange(NTOK)
    row = (n // 64).astype(np.float32)
    col = (n % 64).astype(np.float32)
    inv_freq = (np.float32(10000.0) ** (-np.arange(0, 16, 2, dtype=np.float32) / np.float32(16))).astype(np.float32)
    rope = np.zeros((128, 2, NTOK), np.float32)
    for p in range(128):
        dq = p % 32
        pos = row if dq < 16 else col
        j = dq % 16
        a = (pos * inv_freq[j % 8]).astype(np.float32)
        rope[p, 0] = np.cos(a)
        rope[p, 1] = -np.sin(a) if j < 8 else np.sin(a)
    c["rope"] = rope.astype(bf)
    def dftn(N, kw):
        nn = np.arange(N)
        kk = np.arange(N)
        a = 2 * np.pi * ((nn[:, None] * kk[None, :]) % N) / float(N)
        s = 1.0 / math.sqrt(64.0 * N)
        C = (np.cos(a) * s).reshape(N // 128, 128, N // kw, kw)
        Sn = (np.sin(a) * s).reshape(N // 128, 128, N // kw, kw)
        out = np.stack([C, Sn], axis=0)
        return np.ascontiguousarray(out.transpose(3, 2, 1, 0, 4)).astype(bf)
    c["cn"] = dftn(NTOK, 512)
    c["cn256"] = dftn(LCTX, 256)[0]
    return c


_CONSTS = None


def _get_consts():
    global _CONSTS
    if _CONSTS is None:
        _CONSTS = _consts()
    return _CONSTS


class _Stop(Exception):
    pass


def build_program(nseq=NB_PER_CORE, nlayers=2, dbg=None, stop=None):
    dbg = dbg or {}

    def chk(tag):
        if stop == tag:
            SREF[0].halt = True

    SREF = [None]
    nc = bass.Bass("TRN2", target_bir_lowering=False)

    def din(name, shape, dt=F32):
        return nc.dram_tensor(name, list(shape), dt, kind="ExternalInput").ap()

    x_d = din("x", [nseq, NTOK, D])
    ctx_d = din("ctx", [nseq, LCTX, D])
    cT_d = din("cT", [128, 8, 3])
    wada_d = din("w_ada", [2, D, 6 * D])
    badaT_d = din("b_adaT", [128, 2, 48])
    gT_d = din("gT", [128, 5, 8])
    win_d = din("w_in", [2, D, 2304])
    wout_d = din("w_out", [2, D, D])
    wup_d = din("w_up", [2, D, 2 * DFF])
    wdown_d = din("w_down", [2, DFF, D])
    wsT_d = din("wsT", [128, 2, 4, 128])
    gbiasT_d = din("gbiasT", [128, 2, 2, 128])
    sconvT_d = din("sconvT", [128, 2, 2, 3])
    fconvT_d = din("fconvT", [128, 2, NFF, 3])
    sublnT_d = din("sublnT", [128, 2])
    lamv_d = din("lamv", [128, 2, 4, 32])
    ident_d = din("ident", [128, 128])
    c64_d = din("c64", [128, 2, 128], BF16)
    rperm_d = din("rperm", [128, 128], BF16)
    rope_d = din("rope", [128, 2, NTOK], BF16)
    cn_d = din("cn", [4, 128, 16, 2, 512], BF16)
    cn256_d = din("cn256", [128, 2, 2, 256], BF16)
    out_d = nc.dram_tensor("out", [nseq, NTOK, D], F32, kind="ExternalOutput").ap()
    scr_d = nc.dram_tensor("scr_rows", [2, 512], F32).ap()
    win_f = nc.dram_tensor("win_f", [2, D, 2304], BF16).ap()
    wout_f = nc.dram_tensor("wout_f", [2, D, D], BF16).ap()
    wup_f = nc.dram_tensor("wup_f", [2, D, 2 * DFF], BF16).ap()
    wdown_s = nc.dram_tensor("wdown_s", [2, 8, 128, NFF, 128], BF16).ap()
    dbg_d = {}
    for name, shape in dbg.items():
        dbg_d[name] = nc.dram_tensor("dbg_" + name, list(shape), F32, kind="ExternalOutput").ap()

    lam_init = [0.8 - 0.6 * math.exp(-0.3 * l) for l in range(2)]

    with ExitStack() as st:
        S = Sync(nc, st)
        SREF[0] = S

        tcount = [0]

        def T(stack, name, shape, dt):
            tcount[0] += 1
            return stack.enter_context(nc.sbuf_tensor(f"sb{tcount[0]}_{name}", list(shape), dt))

        ps = st.enter_context(nc.psum_tensor("ps", [128, 8, 512], F32))
        psb = [Buf(f"ps{i}") for i in range(8)]
        outB = Buf("out")
        scr_b = [Buf("scr0"), Buf("scr1")]
        wscr_b = {k: Buf("wscr_" + k) for k in ("in", "out", "up", "down")}
        converted = set()

        def wload(kind, l_, idx, slot, slot_b, scratch_ap, cast_loads):
            key = (kind, l_, idx)
            if S.halt:
                return
            if key not in converted:
                for (dst, src) in cast_loads:
                    S.dma("pool", dst, src, writes=[slot_b])
                S.dma("sp", scratch_ap, slot, reads=[slot_b], writes=[wscr_b[kind]])
                converted.add(key)
            else:
                S.dma("sp", slot, scratch_ap, reads=[wscr_b[kind]], writes=[slot_b])
        dbgB = Buf("dbgout")

        xT_t = T(st, "xT", [128, 8, NTOK], F32)
        xT = TT("xT", xT_t, 8, NTOK, 512)
        xcT_t = T(st, "xcT", [128, 8, LCTX], F32)
        xcT = TT("xcT", xcT_t, 8, LCTX, 256)
        ctxK = T(st, "ctxK", [128, 2, LCTX], BF16); ctxK_b = Buf("ctxK")
        ctxV = T(st, "ctxV", [128, 2, 384], BF16); ctxV_b = Buf("ctxV")
        rope = T(st, "rope", [128, 2, NTOK], BF16); rope_b = Buf("rope")
        ident = T(st, "ident", [128, 128], F32); ident_b = Buf("ident")
        ones_bf = T(st, "ones_bf", [128, 128], BF16); ones_b = Buf("ones")
        coef = T(st, "coef", [128, 2, 2, 128], F32); coef_b = Buf("coef")
        c64 = T(st, "c64", [128, 2, 128], BF16); c64_b = Buf("c64")
        rperm = T(st, "rperm", [128, 128], BF16); rperm_b = Buf("rperm")
        wsT = T(st, "wsT", [128, 2, 4, 128], BF16); wsT_b = Buf("wsT")
        gbias = T(st, "gbias", [128, 2, 2, 128], F32); gbias_b = Buf("gbias")
        sconv = T(st, "sconv", [128, 2, 2, 3], F32); sconv_b = Buf("sconv")
        fconv = T(st, "fconv", [128, 2, NFF, 3], F32); fconv_b = Buf("fconv")
        subln = T(st, "subln", [128, 2], F32); subln_b = Buf("subln")
        gT = T(st, "gT", [128, 5, 8], F32); gT_b = Buf("gT")
        modT = T(st, "modT", [128, 2, 48, 3], F32); mod_b = Buf("modT")
        scg = T(st, "scg", [128, 2, 2, 8, 3], F32)
        epsT = T(st, "epsT", [128, 1], F32); eps_b = Buf("eps")
        neglam = T(st, "neglam", [128, 2], F32)
        zfill = T(st, "zfill", [128, 512], BF16); zfill_b = Buf("zfill")

        pe, act, dve, pool = "pe", "act", "dve", "pool"

        def mm(out, lhsT, rhs, start, stop, reads, writes, inc=None, tp=None):
            if inc is None:
                inc = stop
            kw = {}
            if tp is not None:
                kw["tile_position"] = tp
            return S.op(pe, lambda e: e.matmul(out, lhsT, rhs, start=start, stop=stop, **kw),
                        reads=reads, writes=writes, inc=inc)

        def A(out, in_, func, reads, writes, bias=None, scale=None):
            kw = {}
            if bias is not None:
                kw["bias"] = bias
            if scale is not None:
                kw["scale"] = scale
            return S.op(act, lambda e: e.activation(out, in_, func, **kw), reads=reads, writes=writes)

        def TTop(eng, out, in0, in1, op, reads, writes):
            return S.op(eng, lambda e: e.tensor_tensor(out, in0, in1, op), reads=reads, writes=writes)

        def STT(eng, out, in0, scalar, in1, op0, op1, reads, writes):
            return S.op(eng, lambda e: e.scalar_tensor_tensor(out, in0, scalar, in1, op0, op1),
                        reads=reads, writes=writes)

        def TS(eng, out, in0, s1, s2, op0, op1, reads, writes):
            if op1 is None:
                return S.op(eng, lambda e: e.tensor_scalar(out, in0, s1, None, op0), reads=reads, writes=writes)
            return S.op(eng, lambda e: e.tensor_scalar(out, in0, s1, s2, op0, op1), reads=reads, writes=writes)

        def CP(eng, out, in_, reads, writes):
            if eng == act:
                return S.op(act, lambda e: e.copy(out, in_), reads=reads, writes=writes)
            return S.op(eng, lambda e: e.tensor_copy(out, in_), reads=reads, writes=writes)

        def dump(name, src_ap, reads):
            if name in dbg_d:
                S.dma("pool", dbg_d[name], src_ap, reads=reads, writes=[dbgB])

        conv_b = {}

        def issue_conversions(l_):
            for kind, src_d, dst_d in (("in", win_d, win_f), ("out", wout_d, wout_f), ("up", wup_d, wup_f)):
                conv_b[(kind, l_)] = Buf(f"conv_{kind}_{l_}")
                S.dma("pool", dst_d[l_], src_d[l_], writes=[conv_b[(kind, l_)]])

        def wload_f(kind, l_, slot, slot_b, src_ap):
            if S.halt:
                return
            S.dma("sp", slot, src_ap, reads=[conv_b[(kind, l_)]], writes=[slot_b])

        const_bufs = [rope_b, ident_b, c64_b, rperm_b, gbias_b, sconv_b, fconv_b, subln_b, gT_b]
        S.share_dsem(const_bufs)
        S.dma("sp", rope[:], rope_d, writes=[rope_b])
        S.dma("sp", ident[:], ident_d, writes=[ident_b])
        S.dma("sp", c64[:], c64_d, writes=[c64_b])
        S.dma("sp", rperm[:], rperm_d, writes=[rperm_b])
        S.dma("sp", gbias[:], gbiasT_d, writes=[gbias_b])
        S.dma("sp", sconv[:], sconvT_d, writes=[sconv_b])
        S.dma("sp", fconv[:], fconvT_d, writes=[fconv_b])
        S.dma("sp", subln[:], sublnT_d, writes=[subln_b])
        S.dma("sp", gT[:], gT_d, writes=[gT_b])
        S.dma("pool", wsT[:], wsT_d, writes=[wsT_b])
        issue_conversions(0)
        S.op(dve, lambda e: e.memset(ones_bf[:], 1.0), writes=[ones_b])
        S.op(dve, lambda e: e.memset(zfill[:], 0.0), writes=[zfill_b])
        S.op(dve, lambda e: e.memset(epsT[:], EPS), writes=[eps_b])
        S.op(dve, lambda e: e.memset(coef[:], 1.0), writes=[coef_b])

        with ExitStack() as pro:
            lamv = T(pro, "lamv", [128, 2, 4, 32], F32); lamv_b = Buf("lamv")
            lprod = T(pro, "lprod", [128, 2, 2, 32], F32)
            lsum = T(pro, "lsum", [128, 2, 2], F32)
            lam = T(pro, "lam", [128, 2], F32); lam_b = Buf("lam")
            S.dma("sp", lamv[:], lamv_d, writes=[lamv_b])
            TTop(dve, lprod[:], lamv[:, :, 0::2, :], lamv[:, :, 1::2, :], ALU.mult, [lamv_b], [lam_b])
            S.op(dve, lambda e: e.tensor_reduce(lsum[:], lprod[:], AX.X, ALU.add), reads=[lam_b], writes=[lam_b])
            A(lsum[:], lsum[:], AF.Exp, [lam_b], [lam_b])
            TTop(dve, lam[:], lsum[:, :, 0], lsum[:, :, 1], ALU.subtract, [lam_b], [lam_b])
            for l in range(2):
                TS(dve, lam[:, l:l + 1], lam[:, l:l + 1], -1.0, -lam_init[l], ALU.mult, ALU.add, [lam_b], [lam_b])
                TS(dve, coef[:, l, 1, :], coef[:, l, 1, :], lam[:, l:l + 1], None, ALU.mult, None,
                   [lam_b, coef_b], [coef_b])
                CP(dve, neglam[:, l:l + 1], lam[:, l:l + 1], [lam_b], [coef_b])
                TS(dve, subln[:, l:l + 1], subln[:, l:l + 1], 1.0 - lam_init[l], None, ALU.mult, None,
                   [subln_b], [subln_b])

            cT = T(pro, "cT", [128, 8, 3], F32); cT_b = Buf("cT")
            siluc = T(pro, "siluc", [128, 8, 3], F32)
            badaT = T(pro, "badaT", [128, 2, 48], F32); bada_b = Buf("bada")
            S.dma("sp", cT[:], cT_d, writes=[cT_b])
            S.dma("sp", badaT[:], badaT_d, writes=[bada_b])
            A(siluc[:], cT[:], AF.Silu, [cT_b], [cT_b])
            wa = [T(pro, f"wa{i}", [128, 8, 1024], F32) for i in range(2)]
            wa_b = [Buf(f"wa{i}") for i in range(2)]
            it = 0
            for l in range(nlayers):
                for j in range(6):
                    sl = it % 2
                    src = wada_d[l].rearrange("(k p) c -> p k c", p=128)[:, :, j * 1024:(j + 1) * 1024]
                    S.dma("sp", wa[sl][:], src, writes=[wa_b[sl]])
                    bank = it % 2
                    for cc in range(8):
                        for k in range(8):
                            mm(ps[:, bank, cc * 3:cc * 3 + 3], wa[sl][:, k, cc * 128:(cc + 1) * 128], siluc[:, k, :],
                               k == 0, k == 7, [wa_b[sl], cT_b], [psb[bank]])
                    TTop(dve, modT[:, l, j * 8:(j + 1) * 8, :],
                         ps[:, bank, 0:24].rearrange("p (c s) -> p c s", s=3),
                         badaT[:, l, j * 8:(j + 1) * 8].unsqueeze(2).to_broadcast([128, 8, 3]),
                         ALU.add, [psb[bank], bada_b], [mod_b])
                    it += 1
                for which, off in ((0, 8), (1, 32)):
                    TS(dve, scg[:, l, which], modT[:, l, off:off + 8, :], 1.0, None, ALU.add, None, [mod_b], [mod_b])
                    TTop(dve, scg[:, l, which], scg[:, l, which],
                         gT[:, 2 * which + l, :].unsqueeze(2).to_broadcast([128, 8, 3]), ALU.mult,
                         [mod_b, gT_b], [mod_b])
            S.barrier()
        dump("modT", modT[:], [mod_b])

        def mod(l, j, fc, s):
            return modT[:, l, j * 8 + fc, s:s + 1]

        NB = {}

        def norm_mod(*args):
            for _ in norm_mod_g(*args):
                pass

        def norm_mod_g(src: TT, t0, w, dst_ap_fn, dst_bufs_fn, scA, shA, bank, tmp):
            sqt, tx, rstd_l, tb = tmp
            if not isinstance(rstd_l, list):
                rstd_l = [rstd_l]
            if id(tb) not in NB:
                NB[id(tb)] = [Buf("nm_sq"), [Buf(f"nm_rstd{i}") for i in range(len(rstd_l))],
                              [Buf("nm_tx0"), Buf("nm_tx1")], 0]
            nbe = NB[id(tb)]
            sq_b, tx_b = nbe[0], nbe[2]
            ri_ = nbe[3] % len(rstd_l)
            nbe[3] += 1
            rstd = rstd_l[ri_]
            rs_b = nbe[1][ri_]
            rb = src.bufs(0, 8, t0, t0 + w)
            A(sqt[:, :, 0:w], src.t[:, :, t0:t0 + w], AF.Square, rb, [sq_b])
            yield
            for fc in range(8):
                mm(ps[:, bank, 0:w], ones_bf[:], sqt[:, fc, 0:w], fc == 0, fc == 7, [ones_b, sq_b], [psb[bank]])
            A(rstd[:, 0:w], ps[:, bank, 0:w], AF.Ln, [psb[bank], eps_b], [rs_b], bias=epsT[:, 0:1], scale=1.0 / D)
            A(rstd[:, 0:w], rstd[:, 0:w], AF.Exp, [rs_b], [rs_b], scale=-0.5)
            yield
            for fc in range(8):
                if fc > 0:
                    yield
                eng = dve
                txi = tx[fc % 2]
                TTop(eng, txi[:, 0:w], src.t[:, fc, t0:t0 + w], rstd[:, 0:w], ALU.mult,
                     src.bufs(fc, fc + 1, t0, t0 + w) + [rs_b], [tx_b[fc % 2]])
                if shA is not None:
                    A(dst_ap_fn(fc), txi[:, 0:w], AF.Identity, [tx_b[fc % 2], mod_b], dst_bufs_fn(fc), bias=shA(fc), scale=scA(fc))
                else:
                    A(dst_ap_fn(fc), txi[:, 0:w], AF.Identity, [tx_b[fc % 2], gT_b], dst_bufs_fn(fc), scale=scA(fc))

        def load_seq(src_d, N, dst: TT):
            with ExitStack() as ls:
                stg = [T(ls, f"stg{i}", [128, D], F32) for i in range(4)]
                stg_b = [Buf(f"stg{i}") for i in range(4)]
                nt = N // 128
                grp = min(4, nt)
                bk = 0
                for g0 in range(0, nt, grp):
                    for j in range(grp):
                        S.dma("sp", stg[j][:], src_d[(g0 + j) * 128:(g0 + j + 1) * 128, :], writes=[stg_b[j]])
                    for fc in range(8):
                        bank = bk % 8
                        bk += 1
                        for j in range(grp):
                            S.op(pe, lambda e: e.transpose(ps[:, bank, j * 128:(j + 1) * 128],
                                                           stg[j][:, fc * 128:(fc + 1) * 128], ident[:]),
                                 reads=[stg_b[j], ident_b], writes=[psb[bank]], inc=(j == grp - 1))
                        CP(dve if fc % 2 == 0 else act, dst.t[:, fc, g0 * 128:(g0 + grp) * 128],
                           ps[:, bank, 0:grp * 128], [psb[bank]], dst.bufs(fc, fc + 1, g0 * 128, (g0 + grp) * 128))
                S.barrier()

        def store_seq(b):
            with ExitStack() as ss:
                sqt = T(ss, "f_sq", [128, 8, 512], BF16)
                tx = [T(ss, f"f_tx{i}", [128, 512], F32) for i in range(2)]
                rstd = [T(ss, "f_rstd", [128, 512], F32), T(ss, "f_rstd2", [128, 512], F32)]
                tb = Buf("f_tmp")
                yTs = [T(ss, f"f_yT{i}", [128, 8, 512], F32) for i in range(2)]
                y_bs = [[Buf(f"f_y{j}_{i}") for i in range(8)] for j in range(2)]
                stg = [T(ss, f"f_stg{i}", [128, D], F32) for i in range(2)]
                stg_b = [Buf(f"f_stg{i}") for i in range(2)]
                bk = 0
                si = 0
                for tt in range(NTOK // 512):
                    t0 = tt * 512
                    yT = yTs[tt % 2]
                    y_b = y_bs[tt % 2]
                    norm_mod(xT, t0, 512, lambda fc: yT[:, fc, :], lambda fc: [y_b[fc]],
                             lambda fc: gT[:, 4, fc:fc + 1], None, 7, (sqt, tx, rstd, tb))
                    for j in range(4):
                        sg = si % 2
                        si += 1
                        for half in range(2):
                            bank = bk % 6
                            bk += 1
                            for q in range(4):
                                fc = half * 4 + q
                                S.op(pe, lambda e: e.transpose(ps[:, bank, q * 128:(q + 1) * 128],
                                                               yT[:, fc, j * 128:(j + 1) * 128], ident[:]),
                                     reads=[y_b[fc], ident_b], writes=[psb[bank]], inc=(q == 3))
                            CP(dve if half == 0 else act, stg[sg][:, half * 512:(half + 1) * 512], ps[:, bank, :],
                               [psb[bank]], [stg_b[sg]])
                        S.dma("sp", out_d[b, t0 + j * 128:t0 + (j + 1) * 128, :], stg[sg][:],
                              reads=[stg_b[sg]], writes=[outB])
                S.barrier()

        def layer_pass(l, sidx, N, X: TT, is_ctx, kv_only):
            tw = min(512, N)
            ntt = N // tw
            n128 = N // 128
            nkt = 2 if is_ctx else 18
            koff = 0 if is_ctx else LCTX
            with ExitStack() as M:
                if is_ctx:
                    KT, VA = ctxK, ctxV
                    KT_b = [ctxK_b]
                    VA_b = [ctxV_b]
                    kbuf = lambda kt: ctxK_b
                    vbuf = lambda kt: ctxV_b
                else:
                    REG = T(M, "REG", [128, 15872], BF16)
                    KT = REG[:, 4096:4096 + 4608].rearrange("p (a b) -> p a b", a=2)
                    VA = REG[:, 8704:8704 + 6912].rearrange("p (a b) -> p a b", b=384)
                    KT_b = [Buf(f"KT{i}") for i in range(18)]
                    VA_b = [Buf(f"VA{i}") for i in range(18)]
                    kbuf = lambda kt: KT_b[kt]
                    vbuf = lambda kt: VA_b[kt]
                if is_ctx:
                    REG = T(M, "REGc", [128, 15872], BF16)
                if not kv_only:
                    QT_t = REG[:, 0:2 * N].rearrange("p (a b) -> p a b", a=2)
                    QT = TT("QT", QT_t, 2, N, tw)
                    mixAB_t = T(M, "mixAB", [128, 4, N], BF16)
                    mixAB = TT("mixAB", mixAB_t, 4, N, tw)
                    FT = T(M, "FT", [128, n128, 256], BF16)
                    FT_b = [Buf(f"FT{i}") for i in range(n128)]

                if is_ctx:
                    S.op(pool, lambda e: e.memset(ctxV[:, :, 64:128], 1.0), writes=[ctxV_b])
                    S.op(pool, lambda e: e.memset(ctxV[:, :, 256:320], 1.0), writes=[ctxV_b])

                with ExitStack() as P1:
                    hT_t = T(P1, "hT", [128, 8, N], BF16)
                    hT = TT("hT", hT_t, 8, N, tw)
                    cv_off = [0]

                    def carve(shape, dt):
                        n = 1
                        for d_ in shape[1:]:
                            n *= d_
                        nb = n * (4 if dt == F32 else 2)
                        o = cv_off[0]
                        cv_off[0] += (nb + 63) // 64 * 64
                        assert cv_off[0] <= 15872 * 2, cv_off[0]
                        ap = REG[:, o // 2:(o + nb) // 2]
                        if dt == F32:
                            ap = ap.bitcast(F32)
                        if len(shape) == 3:
                            ap = ap.rearrange("p (a b) -> p a b", a=shape[1])
                        return ap

                    if True:
                        sqt = carve([128, 8, tw], BF16)
                        tx = [carve([128, tw], F32) for i in range(2)]
                        rstd = [carve([128, tw], F32), T(P1, "n_rstd2", [128, tw], F32)]
                        tb = Buf("n_tmp")
                        for tt in range(ntt):
                            t0 = tt * tw
                            norm_mod(X, t0, tw, lambda fc: hT_t[:, fc, t0:t0 + tw],
                                     lambda fc: hT.bufs(fc, fc + 1, t0, t0 + tw),
                                     lambda fc: scg[:, l, 0, fc, sidx:sidx + 1], lambda fc: mod(l, 0, fc, sidx),
                                     tt % 2, (sqt, tx, rstd, tb))
                    dump(f"h_{l}_{sidx}", hT_t[:], hT.bufs(0, 8, 0, N))
                    chk("p1n_" + ("c" if is_ctx else "m"))

                    wsl = [T(P1, f"wsl{i}", [128, 8, 256], BF16) for i in range(3)]
                    wsl_b = [Buf(f"wsl{i}") for i in range(3)]
                    wcnt = [0]

                    def load_group(g):
                        sl = wcnt[0] % 3
                        wcnt[0] += 1
                        src = win_f[l].rearrange("(k p) c -> p k c", p=128)[:, :, g * 256:(g + 1) * 256]
                        wload_f("in", l, wsl[sl][:], wsl_b[sl], src)
                        return sl

                    halfc = [0]

                    def fm_chunk(sl, c):
                        hb = halfc[0] % 2
                        halfc[0] += 1
                        for tt in range(ntt):
                            bank = hb * 4 + tt
                            for k in range(8):
                                mm(ps[:, bank, 0:tw], wsl[sl][:, k, c * 128:(c + 1) * 128],
                                   hT_t[:, k, tt * tw:(tt + 1) * tw], k == 0, k == 7,
                                   [wsl_b[sl]] + hT.bufs(k, k + 1, tt * tw, (tt + 1) * tw), [psb[bank]])
                        return hb

                    def psv(hb):
                        return ps[:, hb * 4:hb * 4 + ntt, 0:tw], psb[hb * 4:hb * 4 + ntt]

                    def v3(ap2d):
                        return ap2d.rearrange("p (a b) -> p a b", b=tw)

                    if kv_only:
                        order = [7, 8]
                    else:
                        order = [0, 1, 2, 4, 3, 5, 6, 7, 8]
                    slots = {}
                    pre = order[:3]
                    for g in pre:
                        slots[g] = load_group(g)
                    nxt = [3]

                    def release_and_prefetch():
                        if nxt[0] < len(order):
                            g = order[nxt[0]]
                            nxt[0] += 1
                            slots[g] = load_group(g)

                    with ExitStack() as tmpm:
                        if not kv_only:
                            sl = slots[0]
                            for c in range(2):
                                hb = fm_chunk(sl, c)
                                pv, pb = psv(hb)
                                A(v3(mixAB_t[:, c, :]), pv, AF.Gelu_apprx_tanh, pb, mixAB.bufs(c, c + 1, 0, N))
                            release_and_prefetch()
                            chk("p1a_" + ("c" if is_ctx else "m"))
                            sl = slots[1]
                            NAV = 4
                            vt = [carve([128, 256], F32) for i in range(NAV)]
                            vs = [carve([128, 256], F32) for i in range(NAV)]
                            ss4 = [carve([128, 4], F32) for i in range(NAV)]
                            vnp = [carve([128, 4, 128], BF16) for i in range(NAV)]
                            gtm = [carve([128, 2, 128], F32) for i in range(NAV)]
                            av_b = [Buf(f"a_b{i}") for i in range(NAV)]
                            vnp_b = [Buf(f"a_vnp{i}") for i in range(NAV)]
                            gt_b = [Buf(f"a_gt{i}") for i in range(NAV)]
                            for i in range(NAV):
                                S.op(pool, lambda e: e.memset(vnp[i], 0.0), writes=[vnp_b[i]])
                            def av_s1(t):
                                i = t % NAV
                                bank = t % 4
                                for k in range(8):
                                    mm(ps[:, bank, 0:256], hT_t[:, k, t * 128:(t + 1) * 128], wsl[sl][:, k, :],
                                       k == 0, k == 7, [wsl_b[sl]] + hT.bufs(k, k + 1, t * 128, (t + 1) * 128),
                                       [psb[bank]])
                                A(vt[i], ps[:, bank, 0:256], AF.Gelu_apprx_tanh, [psb[bank]], [av_b[i]])
                                TTop(dve, vs[i], vt[i], vt[i], ALU.mult, [av_b[i]], [av_b[i]])
                                S.op(dve, lambda e: e.tensor_reduce(ss4[i], vs[i].rearrange("p (h d) -> p h d", d=64),
                                                                    AX.X, ALU.add), reads=[av_b[i]], writes=[av_b[i]])

                            def av_s2(t):
                                i = t % NAV
                                bank2 = 4 + t % 4
                                A(ss4[i], ss4[i], AF.Sqrt, [av_b[i], eps_b], [av_b[i]], bias=epsT[:, 0:1], scale=1.0 / 64)
                                S.op(dve, lambda e: e.reciprocal(ss4[i], ss4[i]), reads=[av_b[i]], writes=[av_b[i]])
                                v4 = vt[i].rearrange("p (h d) -> p h d", d=64)
                                TTop(dve, vnp[i][:, 0::2, 0:64], v4[:, 0::2, :],
                                     ss4[i][:, 0::2].unsqueeze(2).to_broadcast([128, 2, 64]), ALU.mult,
                                     [av_b[i]], [vnp_b[i]])
                                TTop(dve, vnp[i][:, 1::2, 64:128], v4[:, 1::2, :],
                                     ss4[i][:, 1::2].unsqueeze(2).to_broadcast([128, 2, 64]), ALU.mult,
                                     [av_b[i]], [vnp_b[i]])
                                for c in range(2):
                                    for hh in range(2):
                                        mm(ps[:, bank2, c * 128:(c + 1) * 128], vnp[i][:, 2 * c + hh, :],
                                           wsT[:, l, 2 * c + hh, :], hh == 0, hh == 1, [vnp_b[i], wsT_b], [psb[bank2]],
                                           inc=(hh == 1 and c == 1))

                            def av_s3(t):
                                i = t % NAV
                                bank2 = 4 + t % 4
                                TTop(dve, gtm[i], ps[:, bank2, 0:256].rearrange("p (c q) -> p c q", q=128),
                                     gbias[:, l, :, :], ALU.add, [psb[bank2], gbias_b], [gt_b[i]])
                                TTop(dve, mixAB_t[:, 0:2, t * 128:(t + 1) * 128], gtm[i],
                                     mixAB_t[:, 0:2, t * 128:(t + 1) * 128], ALU.mult,
                                     [gt_b[i]] + mixAB.bufs(0, 2, t * 128, (t + 1) * 128),
                                     mixAB.bufs(0, 2, t * 128, (t + 1) * 128))

                            blocks = [list(range(b0, min(b0 + NAV, n128))) for b0 in range(0, n128, NAV)]
                            for bi, blk in enumerate(blocks):
                                if bi == 0:
                                    for t in blk:
                                        av_s1(t)
                                for t in blk:
                                    av_s2(t)
                                if bi + 1 < len(blocks):
                                    for t in blocks[bi + 1]:
                                        av_s1(t)
                                for t in blk:
                                    av_s3(t)
                            release_and_prefetch()
                            chk("p1b_" + ("c" if is_ctx else "m"))
                            slx, slc, slb = slots[2], slots[4], slots[3]
                            S.barrier()
                            cv_off[0] = 0
                            Xb = carve([128, N], BF16); Xb_b = Buf("b_X")
                            cx = carve([128, N], F32); cx_b = Buf("b_cx")
                            yt = [carve([128, tw], F32) for i in range(2)]
                            yt_b = [Buf(f"b_y{i}") for i in range(2)]
                            yi = 0
                            for c in range(2):
                                hb = fm_chunk(slx, c)
                                pv, pb = psv(hb)
                                CP(act, v3(Xb), pv, pb, [Xb_b])
                                hb = fm_chunk(slc, c)
                                pv, pb = psv(hb)
                                TTop(dve, v3(cx), pv, v3(Xb), ALU.mult, pb + [Xb_b], [cx_b])
                                hb = fm_chunk(slb, c)
                                for tt in range(ntt):
                                    t0 = tt * tw
                                    y = yt[yi % 2]; yb = yt_b[yi % 2]; yi += 1
                                    A(y, cx[:, t0:t0 + tw], AF.Identity, [cx_b, sconv_b], [yb], scale=sconv[:, l, c, 1:2])
                                    lo = 1 if t0 == 0 else 0
                                    STT(dve, y[:, lo:tw], cx[:, t0 + lo - 1:t0 + tw - 1], sconv[:, l, c, 0:1], y[:, lo:tw],
                                        ALU.mult, ALU.add, [cx_b, sconv_b, yb], [yb])
                                    hi = tw - 1 if t0 + tw == N else tw
                                    STT(dve, y[:, 0:hi], cx[:, t0 + 1:t0 + hi + 1], sconv[:, l, c, 2:3], y[:, 0:hi],
                                        ALU.mult, ALU.add, [cx_b, sconv_b, yb], [yb])
                                    bank = hb * 4 + tt
                                    TTop(dve, mixAB_t[:, 2 + c, t0:t0 + tw], ps[:, bank, 0:tw], y, ALU.mult,
                                         [psb[bank], yb], mixAB.bufs(2 + c, 3 + c, t0, t0 + tw))
                            release_and_prefetch()
                            release_and_prefetch()
                            release_and_prefetch()
                            chk("p1c_" + ("c" if is_ctx else "m"))
                            sl = slots[5]
                            for t in range(n128):
                                bank = t % 8
                                for k in range(8):
                                    mm(ps[:, bank, 0:256], hT_t[:, k, t * 128:(t + 1) * 128], wsl[sl][:, k, :],
                                       k == 0, k == 7, [wsl_b[sl]] + hT.bufs(k, k + 1, t * 128, (t + 1) * 128),
                                       [psb[bank]])
                                CP(act if t % 2 == 0 else dve, FT[:, t, :], ps[:, bank, 0:256], [psb[bank]], [FT_b[t]])
                            release_and_prefetch()

                        chk("p1d_" + ("c" if is_ctx else "m"))
                        S.barrier()
                        if not is_ctx:
                            S.op(pool, lambda e: e.memset(VA[:, :, 64:128], 1.0), writes=VA_b)
                            S.op(pool, lambda e: e.memset(VA[:, :, 256:320], 1.0), writes=VA_b)
                            CP(pool, KT[:, :, 0:LCTX], ctxK[:], [ctxK_b], KT_b[0:2])
                            CP(pool, VA[:, 0:2, :], ctxV[:], [ctxV_b], VA_b[0:2])
                        chk("p1d1_" + ("c" if is_ctx else "m"))
                        zb = T(tmpm, "r_zb", [128, N], BF16); zb_b = Buf("r_zb")
                        t1 = [T(tmpm, f"r_t1{i}", [128, tw], F32) for i in range(1)]
                        t2 = [T(tmpm, f"r_t2{i}", [128, tw], F32) for i in range(1)]
                        rt_b = [Buf(f"r_t{i}") for i in range(1)]
                        ri = 0
                        for g in ([7] if kv_only else [6, 7]):
                            sl = slots[g]
                            for c in range(2):
                                hb = fm_chunk(sl, c)
                                pv, pb = psv(hb)
                                if g == 6:
                                    dst2d = QT_t[:, c, :]
                                    dstb = lambda a, b_: QT.bufs(c, c + 1, a, b_)
                                else:
                                    dst2d = KT[:, c, koff:koff + N]
                                    if is_ctx:
                                        dstb = lambda a, b_: [ctxK_b]
                                    else:
                                        dstb = lambda a, b_: KT_b[(koff + a) // 128:(koff + b_ - 1) // 128 + 1]
                                if is_ctx:
                                    CP(act, v3(dst2d), pv, pb, dstb(0, N))
                                    continue
                                CP(act, v3(zb[:]), pv, pb, [zb_b])
                                chk("p1d2_" + ("c" if is_ctx else "m"))
                                hb2 = 1 - hb
                                halfc[0] += 1
                                for tt in range(ntt):
                                    mm(ps[:, hb2 * 4 + tt, 0:tw], rperm[:], zb[:, tt * tw:(tt + 1) * tw], True, True,
                                       [rperm_b, zb_b], [psb[hb2 * 4 + tt]])
                                chk("p1d3_" + ("c" if is_ctx else "m"))
                                for tt in range(ntt):
                                    t0 = tt * tw
                                    i = 0; ri += 1
                                    TTop(dve, t1[i][:], ps[:, hb * 4 + tt, 0:tw], rope[:, 0, t0:t0 + tw], ALU.mult,
                                         [psb[hb * 4 + tt], rope_b], [rt_b[i]])
                                    TTop(dve, t2[i][:], ps[:, hb2 * 4 + tt, 0:tw], rope[:, 1, t0:t0 + tw], ALU.mult,
                                         [psb[hb2 * 4 + tt], rope_b], [rt_b[i]])
                                    TTop(dve, dst2d[:, t0:t0 + tw], t1[i][:], t2[i][:], ALU.add, [rt_b[i]], dstb(t0, t0 + tw))
                            release_and_prefetch()
                        chk("p1e_" + ("c" if is_ctx else "m"))
                        sl = slots[8]
                        for t in range(n128):
                            bank = t % 8
                            kt = koff // 128 + t
                            for k in range(8):
                                mm(ps[:, bank, 0:256], hT_t[:, k, t * 128:(t + 1) * 128], wsl[sl][:, k, :],
                                   k == 0, k == 7, [wsl_b[sl]] + hT.bufs(k, k + 1, t * 128, (t + 1) * 128), [psb[bank]])
                            src4 = ps[:, bank, 0:256].rearrange("p (h d) -> p h d", d=64)
                            dst6 = VA[:, kt, :].rearrange("p (b d) -> p b d", d=64)
                            CP(act, dst6[:, 0:3:2, :], src4[:, 0:2, :], [psb[bank]], [vbuf(kt)])
                            CP(dve, dst6[:, 3:6:2, :], src4[:, 2:4, :], [psb[bank]], [vbuf(kt)])
                        S.barrier()
                if kv_only:
                    return
                chk("p1_" + ("c" if is_ctx else "m"))
                dump(f"mixAB_{l}_{sidx}", mixAB_t[:], mixAB.bufs(0, 4, 0, N))
                dump(f"QT_{l}_{sidx}", QT_t[:], QT.bufs(0, 2, 0, N))

                with ExitStack() as P23:
                    mixCD_t = T(P23, "mixCD", [128, 4, N], BF16)
                    mixCD = TT("mixCD", mixCD_t, 4, N, tw)
                    with ExitStack() as P2:
                        ngr = n128 // 4 if not is_ctx else 1
                        gsz = 4 if not is_ctx else 2
                        kw_ = tw
                        P2d = ExitStack()
                        dtile = [T(P2d, f"dft{i}", [128, gsz, 2, kw_], BF16) for i in range(3)]
                        dt_b = [Buf(f"dft{i}") for i in range(3)]
                        U = [T(P2d, f"dU{i}", [128, 4, kw_], BF16) for i in range(2)]
                        U_b = [Buf(f"dU{i}") for i in range(2)]
                        di = 0
                        for ktile in range(ntt):
                            for gi in range(ngr):
                                sl = di % 3; di += 1
                                if is_ctx:
                                    S.dma("sp", dtile[sl][:], cn256_d, writes=[dt_b[sl]])
                                else:
                                    S.dma("sp", dtile[sl][:], cn_d[ktile, :, gi * 4:(gi + 1) * 4, :, :], writes=[dt_b[sl]])
                                for j in range(gsz):
                                    ntile = gi * gsz + j
                                    first = (ntile == 0)
                                    last = (ntile == n128 - 1)
                                    for cs in range(2):
                                        for c in range(2):
                                            bank = cs * 2 + c
                                            mm(ps[:, bank, 0:kw_], FT[:, ntile, c * 128:(c + 1) * 128], dtile[sl][:, j, cs, :],
                                               first, last, [FT_b[ntile], dt_b[sl]], [psb[bank]],
                                               inc=(last and cs == 1 and c == 1) or (j == gsz - 1 and cs == 1 and c == 1))
                            ui = ktile % 2
                            CP(act, U[ui][:, 0:2, :], ps[:, 0:2, 0:kw_], psb[0:2], [U_b[ui]])
                            CP(dve, U[ui][:, 2:4, :], ps[:, 2:4, 0:kw_], psb[2:4], [U_b[ui]])
                            for c in range(2):
                                bank = 4 + (2 * ktile + c) % 4
                                mm(ps[:, bank, 0:kw_], c64[:, 0, :], U[ui][:, c, :], True, False, [c64_b, U_b[ui]], [psb[bank]])
                                mm(ps[:, bank, 0:kw_], c64[:, 1, :], U[ui][:, 2 + c, :], False, True, [c64_b, U_b[ui]], [psb[bank]])
                                CP(act if c == 0 else dve, mixCD_t[:, c, ktile * kw_:(ktile + 1) * kw_], ps[:, bank, 0:kw_],
                                   [psb[bank]], mixCD.bufs(c, c + 1, ktile * kw_, (ktile + 1) * kw_))
                        S.barrier()
                        P2d.close()
                        chk("p2d_" + ("c" if is_ctx else "m"))

                        PTW = [T(P2, f"PT{i}", [128, 2, tw], BF16) for i in range(4)]
                        PT_b = [Buf(f"PT{i}") for i in range(4)]
                        osum = T(P2, "osum", [128, tw], F32); osum_b = Buf("osum")
                        osq = T(P2, "osq", [128, tw], BF16); osq_b = Buf("osq")
                        orst = T(P2, "orst", [128, tw], F32); orst_b = Buf("orst")
                        scl = 32.0 ** -0.5
                        RP = [(0, 1), (2, 3), (6, 7)]
                        obank = [4, 5]
                        steps = [(h, qt, kt) for h in range(4) for qt in range(ntt) for kt in range(nkt)]
                        NS = len(steps)
                        LOOK = 2

                        def hinfo(h):
                            odd = h % 2
                            if odd:
                                vsl = slice(64 + (h // 2) * 192, 64 + (h // 2) * 192 + 128)
                            else:
                                vsl = slice((h // 2) * 192, (h // 2) * 192 + 128)
                            return h // 2, odd, (64 if odd else 0), (0 if odd else 64), vsl

                        def issue_qk_exp(si):
                            h, qt, kt = steps[si]
                            c, odd, pr0, ps_row, vsl = hinfo(h)
                            q0 = qt * tw
                            pair = RP[si % 3]
                            if WARM and not is_ctx:
                                mm(ps[:, pair[0], 0:tw], zfill[:, 0:128], zfill[:, 0:tw], True, True,
                                   [zfill_b], [psb[pair[0]]])
                            for m in range(2):
                                pb = odd * 64 + m * 32
                                mm(ps[:, pair[m], 0:tw], KT[pb:pb + 32, c, kt * 128:(kt + 1) * 128],
                                   QT_t[pb:pb + 32, c, q0:q0 + tw], True, True,
                                   [kbuf(kt)] + QT.bufs(c, c + 1, q0, q0 + tw), [psb[pair[m]]], tp=(pb, 0))
                            i = si % 4
                            A(PTW[i][:, :, 0:tw], ps[:, pair[0]:pair[0] + 2, 0:tw], AF.Exp,
                              [psb[pair[0]], psb[pair[1]]], [PT_b[i]], scale=scl)

                        def issue_av(si):
                            h, qt, kt = steps[si]
                            c, odd, pr0, ps_row, vsl = hinfo(h)
                            i = si % 4
                            for m in range(2):
                                mm(ps[:, obank[m], 0:tw], VA[:, kt, vsl], PTW[i][:, m, 0:tw], kt == 0, kt == nkt - 1,
                                   [vbuf(kt), PT_b[i]], [psb[obank[m]]])

                        osb2 = T(P2, "osb2", [128, 2, tw], F32); osb2_b = Buf("osb2")
                        rrow2 = T(P2, "rrow2", [128, 2, tw], F32); rrow2_b = Buf("rrow2")
                        bcT2 = T(P2, "bcT2", [128, 2, tw], F32); bcT2_b = Buf("bcT2")

                        def epi_a(si):
                            h, qt, kt = steps[si]
                            c, odd, pr0, ps_row, vsl = hinfo(h)
                            pr = slice(pr0, pr0 + 64)
                            rw = slice(ps_row, ps_row + 1)
                            CP(dve, osb2[:, :, 0:tw], ps[:, obank[0]:obank[0] + 2, 0:tw], [psb[obank[0]], psb[obank[1]]], [osb2_b])
                            A(rrow2[rw, :, 0:tw], osb2[rw, :, 0:tw], AF.Ln, [osb2_b], [rrow2_b])
                            A(rrow2[rw, :, 0:tw], rrow2[rw, :, 0:tw], AF.Exp, [rrow2_b], [rrow2_b], scale=-1.0)
                            S.dma("sp", scr_d[:, 0:tw].unsqueeze(0), rrow2[rw, :, 0:tw], reads=[rrow2_b], writes=[scr_b[0]])
                            S.dma("sp", bcT2[pr, :, 0:tw], scr_d[:, 0:tw].unsqueeze(0).to_broadcast([64, 2, tw]),
                                  reads=[scr_b[0]], writes=[bcT2_b])

                        def epi_b(si, sj):
                            h, qt, kt = steps[si]
                            c, odd, pr0, ps_row, vsl = hinfo(h)
                            q0 = qt * tw
                            pr = slice(pr0, pr0 + 64)
                            TTop(dve, osb2[pr, 0, 0:tw], osb2[pr, 0, 0:tw], bcT2[pr, 0, 0:tw], ALU.mult,
                                 [osb2_b, bcT2_b], [osb2_b])
                            STT(dve, osb2[pr, 1, 0:tw], osb2[pr, 1, 0:tw], neglam[pr, l:l + 1], bcT2[pr, 1, 0:tw],
                                ALU.mult, ALU.mult, [osb2_b, bcT2_b, coef_b], [osb2_b])
                            TTop(dve, osum[pr, 0:tw], osb2[pr, 0, 0:tw], osb2[pr, 1, 0:tw], ALU.add,
                                 [osb2_b], [osum_b])
                            TTop(dve, osq[pr, 0:tw], osum[pr, 0:tw], osum[pr, 0:tw], ALU.mult, [osum_b], [osq_b])
                            bank = RP[sj % 3][0]
                            mm(ps[:, bank, 0:tw], ones_bf[pr, :], osq[pr, 0:tw], True, True, [ones_b, osq_b], [psb[bank]])
                            A(orst[pr, 0:tw], ps[pr, bank, 0:tw], AF.Ln, [psb[bank], eps_b], [orst_b],
                              bias=epsT[pr, 0:1], scale=1.0 / 64)
                            A(orst[pr, 0:tw], orst[pr, 0:tw], AF.Exp, [orst_b], [orst_b], scale=-0.5)
                            STT(dve, mixCD_t[pr, 2 + c, q0:q0 + tw], osum[pr, 0:tw], subln[pr, l:l + 1], orst[pr, 0:tw],
                                ALU.mult, ALU.mult, [osum_b, orst_b, subln_b], mixCD.bufs(2 + c, 3 + c, q0, q0 + tw))

                        LAG = min(12, nkt - 1)
                        pend = {}
                        for si in range(min(LOOK, NS)):
                            issue_qk_exp(si)
                        for si in range(NS):
                            if si + LOOK < NS:
                                issue_qk_exp(si + LOOK)
                            issue_av(si)
                            if si in pend:
                                epi_b(pend.pop(si), si)
                            if steps[si][2] == nkt - 1:
                                epi_a(si)
                                if si + LAG < NS:
                                    pend[si + LAG] = si
                                else:
                                    epi_b(si, si)
                        S.barrier()
                    dump(f"mixCD_{l}_{sidx}", mixCD_t[:], mixCD.bufs(0, 4, 0, N))
                    chk("p2_" + ("c" if is_ctx else "m"))

                    with ExitStack() as P3:
                        wo = [T(P3, f"wo{i}", [128, 8, 256], BF16) for i in range(2)]
                        wo_b = [Buf(f"wo{i}") for i in range(2)]
                        bk = 0
                        for fo in range(8):
                            sl = (fo // 2) % 2
                            if fo % 2 == 0:
                                src = wout_f[l].rearrange("(k p) c -> p k c", p=128)[:, :, fo * 128:(fo + 2) * 128]
                                wload_f("out", l, wo[sl][:], wo_b[sl], src)
                            wcol = (fo % 2) * 128
                            for tt in range(ntt):
                                t0 = tt * tw
                                bank = bk % 8; bk += 1
                                for k in range(8):
                                    if k < 4:
                                        rhs = mixAB_t[:, k, t0:t0 + tw]; rb = mixAB.bufs(k, k + 1, t0, t0 + tw)
                                    else:
                                        rhs = mixCD_t[:, k - 4, t0:t0 + tw]; rb = mixCD.bufs(k - 4, k - 3, t0, t0 + tw)
                                    mm(ps[:, bank, 0:tw], wo[sl][:, k, wcol:wcol + 128], rhs, k == 0, k == 7, [wo_b[sl]] + rb, [psb[bank]])
                                xb_ = X.bufs(fo, fo + 1, t0, t0 + tw)
                                STT(dve, X.t[:, fo, t0:t0 + tw], ps[:, bank, 0:tw], mod(l, 2, fo, sidx), X.t[:, fo, t0:t0 + tw],
                                    ALU.mult, ALU.add, [psb[bank], mod_b] + xb_, xb_)
                        S.barrier()
            dump(f"xmid_{l}_{sidx}", X.t[:], X.bufs(0, 8, 0, N))
            chk("p3_" + ("c" if is_ctx else "m"))

            with ExitStack() as P4:
                stw = min(1024, N)
                nst = N // stw
                nsub = stw // tw
                g_t = T(P4, "g_t", [128, NFF, stw], BF16)
                g_b = [[Buf(f"g{j}_{s}") for s in range(nsub)] for j in range(NFF)]
                h2 = T(P4, "h2", [128, 8, stw + 1], BF16)
                h2_b = [Buf(f"h2_{s}") for s in range(nsub + 1)]
                sqt = T(P4, "n_sq", [128, 8, tw], BF16)
                tx = [T(P4, f"n_tx{i}", [128, tw], F32) for i in range(2)]
                rstd = T(P4, "n_rstd", [128, tw], F32)
                tb = Buf("n_tmp")
                wu = [T(P4, f"wu{i}", [128, 8, 512], BF16) for i in range(2)]
                wu_b = [Buf(f"wu{i}") for i in range(2)]
                wd = [T(P4, f"wd{i}", [128, NFF, 128], BF16) for i in range(2)]
                wd_b = [Buf(f"wd{i}") for i in range(2)]
                yv = [T(P4, f"f_y{i}", [128, stw], F32) for i in range(2)]
                yv_b = [Buf(f"f_y{i}") for i in range(2)]
                HB = 7
                wui = 0
                wdi = 0
                h2h = T(P4, "h2h", [128, 8, 2], BF16)
                h2h_b = Buf("h2h")
                if nst > 1:
                    for sti in range(nst):
                        hcol = sti * stw + stw if sti == 0 else sti * stw - 1
                        norm_mod(X, hcol, 1, lambda fc: h2h[:, fc, sti:sti + 1], lambda fc: [h2h_b],
                                 lambda fc: scg[:, l, 1, fc, sidx:sidx + 1], lambda fc: mod(l, 3, fc, sidx),
                                 2 + sti, (sqt, tx, rstd, tb))
                for sti in range(nst):
                    s0 = sti * stw
                    halo = None
                    if nst > 1:
                        halo = s0 + stw if sti == 0 else s0 - 1
                    def h2_gens(s0_):
                        gs = []
                        for sb_ in range(nsub):
                            gs.append(norm_mod_g(X, s0_ + sb_ * tw, tw,
                                                 (lambda sb__: (lambda fc: h2[:, fc, sb__ * tw:(sb__ + 1) * tw]))(sb_),
                                                 (lambda sb__: (lambda fc: [h2_b[sb__]]))(sb_),
                                                 lambda fc: scg[:, l, 1, fc, sidx:sidx + 1], lambda fc: mod(l, 3, fc, sidx),
                                                 7, (sqt, tx, rstd, tb)))
                        return gs
                    if sti == 0:
                        for g_ in h2_gens(s0):
                            for _ in g_:
                                pass
                    bring = 0
                    for j in range(NFF):
                        if j % 2 == 0:
                            wui += 1
                            sl = wui % 2
                            srcw = wup_f[l].rearrange("(k p) c -> p k c", p=128)
                            wload_f("up", l, wu[sl][:, :, 0:256], wu_b[sl], srcw[:, :, j * 128:(j + 2) * 128])
                            wload_f("up", l, wu[sl][:, :, 256:512], wu_b[sl], srcw[:, :, DFF + j * 128:DFF + (j + 2) * 128])
                        sl = wui % 2
                        ac0 = (j % 2) * 128
                        bc0 = 256 + (j % 2) * 128
                        ab = (j % 2) * 2
                        abanks = [ab + s for s in range(nsub)]
                        for sb in range(nsub):
                            for k in range(8):
                                mm(ps[:, abanks[sb], 0:tw], wu[sl][:, k, ac0:ac0 + 128], h2[:, k, sb * tw:(sb + 1) * tw],
                                   k == 0, k == 7, [wu_b[sl], h2_b[sb]], [psb[abanks[sb]]])
                        if halo is not None:
                            for k in range(8):
                                mm(ps[:, HB, j:j + 1], wu[sl][:, k, ac0:ac0 + 128], h2h[:, k, sti:sti + 1], k == 0, k == 7,
                                   [wu_b[sl], h2h_b], [psb[HB]])
                        bbanks = []
                        for sb in range(nsub):
                            bb = 4 + bring % 3; bring += 1
                            bbanks.append(bb)
                            for k in range(8):
                                mm(ps[:, bb, 0:tw], wu[sl][:, k, bc0:bc0 + 128], h2[:, k, sb * tw:(sb + 1) * tw],
                                   k == 0, k == 7, [wu_b[sl], h2_b[sb]], [psb[bb]])
                        i = j % 2
                        y = yv[i]; yb = yv_b[i]
                        apv = ps[:, ab:ab + nsub, 0:tw]
                        apb = [psb[b_] for b_ in abanks]
                        a2d = lambda lo_, hi_: None
                        y3 = y[:, 0:stw].rearrange("p (a b) -> p a b", b=tw)
                        A(y3, apv, AF.Identity, apb + [fconv_b], [yb], scale=fconv[:, l, j, 1:2])
                        for sb in range(nsub):
                            lo = 1 if sb == 0 else 0
                            if sb == 0:
                                STT(dve, y[:, 1:tw], ps[:, abanks[0], 0:tw - 1], fconv[:, l, j, 0:1], y[:, 1:tw],
                                    ALU.mult, ALU.add, [psb[abanks[0]], yb, fconv_b], [yb])
                            else:
                                STT(dve, y[:, sb * tw + 1:(sb + 1) * tw], ps[:, abanks[sb], 0:tw - 1], fconv[:, l, j, 0:1],
                                    y[:, sb * tw + 1:(sb + 1) * tw], ALU.mult, ALU.add, [psb[abanks[sb]], yb, fconv_b], [yb])
                                STT(dve, y[:, sb * tw:sb * tw + 1], ps[:, abanks[sb - 1], tw - 1:tw], fconv[:, l, j, 0:1],
                                    y[:, sb * tw:sb * tw + 1], ALU.mult, ALU.add, [psb[abanks[sb - 1]], yb, fconv_b], [yb])
                            STT(dve, y[:, sb * tw:(sb + 1) * tw - 1], ps[:, abanks[sb], 1:tw], fconv[:, l, j, 2:3],
                                y[:, sb * tw:(sb + 1) * tw - 1], ALU.mult, ALU.add, [psb[abanks[sb]], yb, fconv_b], [yb])
                            if sb + 1 < nsub:
                                STT(dve, y[:, (sb + 1) * tw - 1:(sb + 1) * tw], ps[:, abanks[sb + 1], 0:1], fconv[:, l, j, 2:3],
                                    y[:, (sb + 1) * tw - 1:(sb + 1) * tw], ALU.mult, ALU.add,
                                    [psb[abanks[sb + 1]], yb, fconv_b], [yb])
                        if halo is not None:
                            if sti == 0:
                                STT(dve, y[:, stw - 1:stw], ps[:, HB, j:j + 1], fconv[:, l, j, 2:3], y[:, stw - 1:stw],
                                    ALU.mult, ALU.add, [psb[HB], yb, fconv_b], [yb])
                            else:
                                STT(dve, y[:, 0:1], ps[:, HB, j:j + 1], fconv[:, l, j, 0:1], y[:, 0:1],
                                    ALU.mult, ALU.add, [psb[HB], yb, fconv_b], [yb])
                        A(y[:, 0:stw], y[:, 0:stw], AF.Silu, [yb], [yb])
                        for sb in range(nsub):
                            TTop(dve, g_t[:, j, sb * tw:(sb + 1) * tw], ps[:, bbanks[sb], 0:tw], y[:, sb * tw:(sb + 1) * tw],
                                 ALU.mult, [psb[bbanks[sb]], yb], [g_b[j][sb]])
                    bk = 0
                    nxt_gens = h2_gens(s0 + stw) if sti + 1 < nst else []
                    def advance(n):
                        for _ in range(n):
                            while nxt_gens:
                                try:
                                    next(nxt_gens[0])
                                    break
                                except StopIteration:
                                    nxt_gens.pop(0)
                    for fo in range(8):
                        sl = wdi % 2; wdi += 1
                        srcd = wdown_d[l].rearrange("(j p) c -> p j c", p=128)[:, :, fo * 128:(fo + 1) * 128]
                        wload("down", l, fo, wd[sl][:], wd_b[sl], wdown_s[l, fo], [(wd[sl][:], srcd)])
                        for sb in range(nsub):
                            bank = bk % 7; bk += 1
                            for j in range(NFF):
                                mm(ps[:, bank, 0:tw], wd[sl][:, j, :], g_t[:, j, sb * tw:(sb + 1) * tw], j == 0, j == NFF - 1,
                                   [wd_b[sl], g_b[j][sb]], [psb[bank]])
                            t0 = s0 + sb * tw
                            xb_ = X.bufs(fo, fo + 1, t0, t0 + tw)
                            STT(dve, X.t[:, fo, t0:t0 + tw], ps[:, bank, 0:tw], mod(l, 5, fo, sidx), X.t[:, fo, t0:t0 + tw],
                                ALU.mult, ALU.add, [psb[bank], mod_b] + xb_, xb_)
                            advance(2)
                    advance(1000)
                S.barrier()
            dump(f"x_{l}_{sidx}", X.t[:], X.bufs(0, 8, 0, N))

        if True:
            chk("pro")
            for b in range(nseq):
                load_seq(x_d[b], NTOK, xT)
                load_seq(ctx_d[b], LCTX, xcT)
                chk("load")
                for l in range(nlayers):
                    layer_pass(l, 2, LCTX, xcT, True, kv_only=(l == 1))
                    chk("ctx")
                    if b == 0 and l == 0 and nlayers > 1:
                        issue_conversions(1)
                    layer_pass(l, b, NTOK, xT, False, kv_only=False)
                store_seq(b)
        S.halt = False
        for e_ in S.pending:
            S.pending[e_] = []
        S.barrier()
        S.final_wait("sp", [outB, dbgB])
        build_program.stats = dict(S.ninst)
        build_program.nsem = S.nsem
    return nc


def _fm(v):
    v = np.asarray(v, np.float32)
    k = v.shape[-1] // 128
    r = v.reshape(v.shape[:-1] + (k, 128))
    return np.ascontiguousarray(np.moveaxis(r, -1, 0))


def make_in_maps(inputs, nseq=NB_PER_CORE, ncores=N_CORES):
    C = _get_consts()
    f32 = lambda a: np.ascontiguousarray(np.asarray(a, np.float32))
    shared = {
        "w_ada": f32(inputs["w_ada"]), "w_in": f32(inputs["w_in"]), "w_out": f32(inputs["w_out"]),
        "w_up": f32(inputs["ffn_w_up"]), "w_down": f32(inputs["ffn_w_down"]),
        "b_adaT": _fm(inputs["b_ada"]),
        "gT": _fm(np.concatenate([f32(inputs["norm1_g"]), f32(inputs["norm2_g"]), f32(inputs["final_g"])[None]], 0)),
        "wsT": np.ascontiguousarray(f32(inputs["gmlp_ws"]).transpose(3, 0, 1, 2)),
        "gbiasT": np.ascontiguousarray(
            np.repeat(f32(inputs["gmlp_bs"]).reshape(2, 2, 2, 1, 128), 64, axis=3).reshape(2, 2, 128, 128).transpose(2, 0, 1, 3)),
        "sconvT": np.ascontiguousarray(f32(inputs["sconv_w"]).reshape(2, 3, 2, 128).transpose(3, 0, 2, 1)),
        "fconvT": np.ascontiguousarray(f32(inputs["ffn_conv_w"]).reshape(2, 3, NFF, 128).transpose(3, 0, 2, 1)),
        "sublnT": np.ascontiguousarray(np.tile(f32(inputs["subln_g"]), (1, 2)).T),
        "lamv": np.ascontiguousarray(np.broadcast_to(
            np.stack([f32(inputs["lambda_q1"]), f32(inputs["lambda_k1"]), f32(inputs["lambda_q2"]), f32(inputs["lambda_k2"])], 1)[None],
            (128, 2, 4, 32))),
        "ident": C["ident"], "c64": C["c64"], "rperm": C["rperm"], "rope": C["rope"], "cn": C["cn"], "cn256": C["cn256"],
    }
    x = f32(inputs["x"]); ctx = f32(inputs["ctx"]); c = f32(inputs["c"]); cc = f32(inputs["c_ctx"])
    maps = []
    for i in range(ncores):
        m = dict(shared)
        m["x"] = x[i * nseq:(i + 1) * nseq]
        m["ctx"] = ctx[i * nseq:(i + 1) * nseq]
        cv = np.stack([c[i * nseq + (s % nseq)] for s in range(2)] + [cc], 0)
        m["cT"] = np.ascontiguousarray(cv.reshape(3, 8, 128).transpose(2, 1, 0))
        maps.append(m)
    return maps


_NC_CACHE = {}


def kernel(**inputs):
    nc = build_program()
    maps = make_in_maps(inputs)
    res = run_bass_kernel_spmd(nc, maps, core_ids=list(range(N_CORES)))
    out = np.concatenate([np.asarray(r["out"], np.float32) for r in res.results], axis=0)
    return out
```
